# Optimizing a Trainium2 kernel written in Bass

```python
import jax, jax.numpy as jnp
from jax import lax
import numpy as np

D_MODEL = 2048
BATCH = 2
SEQ = 8192
DEPTH = 1

CHUNK = 64
D_MIX = D_MODEL
D_CONV = D_MIX // 2
CONV_GROUPS = 8
CONV_WIDTH = 3
D_GLA_V = D_MIX - D_CONV
GLA_HEADS = 4
GLA_DV = D_GLA_V // GLA_HEADS
GLA_DK = GLA_DV // 2
D_GLA_K = GLA_HEADS * GLA_DK
GATE_RANK = 16
GATE_NORMALIZER = 16.0
D_FF = 4 * D_MODEL
EPS = 1e-6
IN_SIZES = (D_CONV, D_CONV, D_CONV, D_GLA_K, D_GLA_K, D_GLA_V, D_GLA_V, GATE_RANK)
D_IN = sum(IN_SIZES)

kernel_name = "hymba_conv_gla_sqrelu_block"


def rmsnorm(x, g):
    xf = x.astype(jnp.float32)
    y = xf * lax.rsqrt(jnp.mean(xf * xf, axis=-1, keepdims=True) + EPS)
    return (y * g.astype(jnp.float32)).astype(x.dtype)


def group_rms(x, groups):
    xf = x.astype(jnp.float32).reshape(x.shape[:-1] + (groups, x.shape[-1] // groups))
    xf = xf * lax.rsqrt(jnp.mean(xf * xf, axis=-1, keepdims=True) + EPS)
    return xf.reshape(x.shape)


def short_conv_mixer(b_gate, c_gate, h, conv_w, conv_g):
    seq = h.shape[1]
    u = c_gate * h
    up = jnp.pad(u, ((0, 0), (CONV_WIDTH - 1, 0), (0, 0)))
    conv = sum(up[:, k:k + seq, :] * conv_w[:, k] for k in range(CONV_WIDTH))
    y = b_gate * conv
    return (group_rms(y, CONV_GROUPS) * conv_g.astype(jnp.float32)).astype(h.dtype)


def gla_chunk_causal(q, k, v, log_a):
    bsz, seq, nh, dk = q.shape
    dv = v.shape[-1]
    nc = seq // CHUNK
    f32 = jnp.float32
    q = q.astype(f32).reshape(bsz, nc, CHUNK, nh, dk) * (dk ** -0.5)
    k = k.astype(f32).reshape(bsz, nc, CHUNK, nh, dk)
    v = v.astype(f32).reshape(bsz, nc, CHUNK, nh, dv)
    la = log_a.astype(f32).reshape(bsz, nc, CHUNK, nh, dk)
    b_cum = jnp.cumsum(la, axis=2)
    b_end = b_cum[:, :, -1:]
    k_dec = k * jnp.exp(b_end - b_cum)
    kv = jnp.einsum('bnshk,bnshv->bnhkv', k_dec, v)
    decay = jnp.exp(b_end[:, :, 0])

    def step(state, xs):
        q_c, dec_c, kv_c = xs
        state = dec_c[..., None] * state + kv_c
        o_c = jnp.einsum('bthk,bhkv->bthv', q_c, state)
        return state, o_c

    s0 = jnp.zeros((bsz, nh, dk, dv), f32)
    xs = (jnp.moveaxis(q, 1, 0), jnp.moveaxis(decay, 1, 0), jnp.moveaxis(kv, 1, 0))
    _, o = lax.scan(step, s0, xs)
    return jnp.moveaxis(o, 0, 1).reshape(bsz, seq, nh, dv)


def setup_inputs(seed: int = 0) -> dict:
    key = jax.random.key(seed)
    ks = jax.random.split(key, 14)
    f32 = jnp.float32
    nrm = lambda k, shape, s: jax.random.normal(k, shape, f32) * s
    gain = lambda k, shape: 1.0 + 0.02 * jax.random.normal(k, shape, f32)
    return {
        "x": jax.random.normal(ks[0], (BATCH, SEQ, D_MODEL), f32),
        "norm1_g": gain(ks[1], (DEPTH, D_MODEL)),
        "w_in": nrm(ks[2], (DEPTH, D_MODEL, D_IN), D_MODEL ** -0.5),
        "w_gate_up": nrm(ks[3], (DEPTH, GATE_RANK, D_GLA_K), GATE_RANK ** -0.5),
        "b_gate": nrm(ks[4], (DEPTH, D_GLA_K), 0.1),
        "conv_w": nrm(ks[5], (DEPTH, D_CONV, CONV_WIDTH), CONV_WIDTH ** -0.5),
        "conv_norm_g": gain(ks[6], (DEPTH, D_CONV)),
        "gla_norm_g": gain(ks[7], (DEPTH, GLA_DV)),
        "w_out": nrm(ks[8], (DEPTH, D_MIX, D_MODEL), D_MIX ** -0.5),
        "norm2_g": gain(ks[9], (DEPTH, D_MODEL)),
        "w_ff1": nrm(ks[10], (DEPTH, D_MODEL, D_FF), D_MODEL ** -0.5),
        "w_ff2": nrm(ks[11], (DEPTH, D_FF, D_MODEL), D_FF ** -0.5),
        "norm_f_g": gain(ks[12], (D_MODEL,)),
    }


def reference(x, norm1_g, w_in, w_gate_up, b_gate, conv_w, conv_norm_g, gla_norm_g,
              w_out, norm2_g, w_ff1, w_ff2, norm_f_g):
    bsz, seq, _ = x.shape
    split_at = [int(i) for i in np.cumsum(IN_SIZES)[:-1]]
    for l in range(DEPTH):
        u = rmsnorm(x, norm1_g[l])
        z = u @ w_in[l]
        cb, cc, ch, q, k, v, og, a_low = jnp.split(z, split_at, axis=-1)
        y_conv = short_conv_mixer(cb, cc, ch, conv_w[l], conv_norm_g[l])
        log_a = jax.nn.log_sigmoid(a_low @ w_gate_up[l] + b_gate[l]) / GATE_NORMALIZER
        o = gla_chunk_causal(q.reshape(bsz, seq, GLA_HEADS, GLA_DK),
                             k.reshape(bsz, seq, GLA_HEADS, GLA_DK),
                             v.reshape(bsz, seq, GLA_HEADS, GLA_DV),
                             log_a.reshape(bsz, seq, GLA_HEADS, GLA_DK))
        o = o * lax.rsqrt(jnp.mean(o * o, axis=-1, keepdims=True) + EPS)
        o = o * gla_norm_g[l].astype(jnp.float32) * jax.nn.silu(
            og.astype(jnp.float32).reshape(bsz, seq, GLA_HEADS, GLA_DV))
        y_gla = o.reshape(bsz, seq, D_GLA_V).astype(x.dtype)
        y = jnp.concatenate([y_conv, y_gla], axis=-1)
        x = x + y @ w_out[l]
        h = rmsnorm(x, norm2_g[l])
        x = x + jnp.square(jax.nn.relu(h @ w_ff1[l])) @ w_ff2[l]
    return rmsnorm(x, norm_f_g)
```

```python
import numpy as np
from contextlib import ExitStack
import concourse.bass as bass
import concourse.mybir as mybir
from concourse.bass_utils import run_bass_kernel_spmd

F32 = mybir.dt.float32
BF16 = mybir.dt.bfloat16
AF = mybir.ActivationFunctionType
ALU = mybir.AluOpType

NCORE = 8
D = 2048
DIN = 6160
DFF = 8192
TOK = 2048
TT = 1024
KC = 16
EPS = 1e-6
NB = 2


class Op:
    __slots__ = ("sem", "val")

    def __init__(self, sem, val):
        self.sem = sem
        self.val = val


class Buf:
    __slots__ = ("name", "w", "r", "excl")

    def __init__(self, name, excl=False):
        self.name = name
        self.w = None
        self.r = {}
        self.excl = excl


class _Eng:
    def __init__(self, name, sem):
        self.name = name
        self.sem = sem
        self.count = 0
        self.prog = []
        self.waited = {}


class Prog:
    ENGS = ("pe", "act", "dve", "pool", "sp")

    def __init__(self, sems):
        self.eng = {n: _Eng(n, sems[n]) for n in self.ENGS}
        self.dma_count = {}
        self.bar = {}

    @staticmethod
    def _need(need, op):
        if op is None:
            return
        k = id(op.sem)
        if k not in need or need[k].val < op.val:
            need[k] = op

    def emit(self, eng, fn, reads=(), writes=(), deps=(), signal=True, dma_sem=None, ndma=1):
        E = self.eng[eng]
        need = {}
        xr = [b for b in reads if b.excl]
        if xr:
            reads = [b for b in reads if not b.excl]
            writes = list(writes) + xr
        for b in reads:
            self._need(need, b.w)
        for b in writes:
            self._need(need, b.w)
            for r in b.r.values():
                self._need(need, r)
        for d in deps:
            self._need(need, d)
        for d in self.bar.values():
            self._need(need, d)
        for k, op in need.items():
            if eng == "pe" and op.sem is E.sem:
                continue
            if E.waited.get(k, 0) < op.val:
                E.waited[k] = op.val
                E.prog.append(("wait", op.sem, op.val))
        if dma_sem is not None:
            k = id(dma_sem)
            self.dma_count[k] = self.dma_count.get(k, 0) + 16 * ndma
            op = Op(dma_sem, self.dma_count[k])
            E.prog.append(("dma", fn, dma_sem))
        elif signal:
            E.count += 1
            op = Op(E.sem, E.count)
            E.prog.append(("sig", fn, E.sem))
        else:
            op = None
            E.prog.append(("raw", fn, None))
        if op is not None:
            k = id(op.sem)
            for b in reads:
                if k not in b.r or b.r[k].val < op.val:
                    b.r[k] = op
            for b in writes:
                b.w = op
                b.r = {}
        return op

    def barrier(self, extra=()):
        self.bar = {}
        for E in self.eng.values():
            if E.name in ("sp", "pool"):
                continue
            if E.count > 0:
                self._need(self.bar, Op(E.sem, E.count))
        for op in extra:
            self._need(self.bar, op)

    def run_engine(self, name, e):
        for item in self.eng[name].prog:
            kind = item[0]
            if kind == "wait":
                e.wait_ge(item[1], item[2])
            elif kind == "dma":
                r = item[1](e)
                if isinstance(r, (list, tuple)):
                    for ins in r:
                        ins.then_inc(item[2], 16)
                else:
                    r.then_inc(item[2], 16)
            elif kind == "sig":
                item[1](e).then_inc(item[2], 1)
            else:
                item[1](e)


def build_program():
    nc = bass.Bass("TRN2", target_bir_lowering=False)

    def din(name, shape):
        return nc.dram_tensor(name, shape, F32, kind="ExternalInput").ap()

    x = din("x", [TOK, D])
    xh = din("xh", [128, D])
    norm1_g = din("norm1_g", [D])
    w_in = din("w_in", [D, DIN])
    w_gate_up = din("w_gate_up", [16, 512])
    b_gate = din("b_gate", [1, 512])
    conv_w = din("conv_w", [1024, 3])
    conv_norm_g = din("conv_norm_g", [1024])
    gla_norm_g = din("gla_norm_g", [1, 256])
    w_out = din("w_out", [D, D])
    norm2_g = din("norm2_g", [D])
    w_ff1 = din("w_ff1", [D, DFF])
    w_ff2 = din("w_ff2", [DFF, D])
    norm_f_g = din("norm_f_g", [1, D])
    cmask_d = din("cmask", [128, 8])
    out = nc.dram_tensor("out", [TOK, D], F32, kind="ExternalOutput").ap()
    xin = nc.dram_tensor("xch_in", [128, 1032], F32)
    xout = nc.dram_tensor("xch_out", [128 * NCORE, 1032], F32)

    with ExitStack() as es:
        def sb(name, shape, dt=F32):
            return es.enter_context(nc.sbuf_tensor(name, shape, dt))

        slab = [sb(f"slab{i}", [128, KC, 512], BF16) for i in range(NB)]
        uT = sb("uT", [128, KC, TT], BF16)
        yT = sb("yT", [128, KC, TT], BF16)
        arena = sb("arena", [128, 16384], F32)
        xt = [sb(f"xt{i}", [128, D], F32) for i in range(2)]
        xb = [sb(f"xb{i}", [128, D], BF16) for i in range(2)]
        Sx = sb("Sx", [128, 1032], F32)
        Sb_ = sb("Sb", [128, 4, 256], BF16)
        gf_rep = sb("gf_rep", [128, D], F32)
        glag_rep = sb("glag_rep", [128, 256], F32)
        wg_ext = sb("wg_ext", [128, 512], F32)
        alT = sb("alT", [128, 512], F32)
        identf = sb("identf", [128, 128], F32)
        identb = sb("identb", [128, 128], BF16)
        ones128 = sb("ones128", [128, 128], F32)
        Mrev = sb("Mrev", [128, 128], F32)
        Ind = sb("Ind", [128, 2], F32)
        g1T = sb("g1T", [128, KC], F32)
        g2T = sb("g2T", [128, KC], F32)
        convw = sb("convw", [128, 8, 3], F32)
        convg = sb("convg", [128, 8], F32)
        cmask = sb("cmask_sb", [128, 8], F32)
        halo = sb("halo", [128, 8, 2], F32)
        stats = sb("stats", [128, 64], F32)

        ps = [es.enter_context(nc.psum_tensor(f"ps{i}", [128, 512], F32)) for i in range(8)]
        psb = [p[:, :].bitcast(BF16) for p in ps]

        sems = {n: es.enter_context(nc.semaphore(f"s_{n}")) for n in Prog.ENGS}
        sem_slab = [es.enter_context(nc.semaphore(f"s_slab{i}")) for i in range(NB)]
        sem_xt = [es.enter_context(nc.semaphore(f"s_xt{i}")) for i in range(2)]
        sem_xr = [es.enter_context(nc.semaphore(f"s_xr{i}")) for i in range(8)]
        sem_st = [es.enter_context(nc.semaphore(f"s_st{i}")) for i in range(2)]
        sem_c = [es.enter_context(nc.semaphore(f"s_c{i}")) for i in range(4)]
        sem_cc = es.enter_context(nc.semaphore("s_cc"))
        block = es.enter_context(nc.Block())

        P = Prog(sems)

        B_slab = [Buf(f"slab{i}") for i in range(NB)]
        B_uT = [Buf(f"uT{i}") for i in range(8)]
        B_yTc = [Buf(f"yTc{i}") for i in range(KC)]
        B_ps = [Buf(f"ps{i}", excl=True) for i in range(8)]
        B_xt = [Buf(f"xt{i}") for i in range(2)]
        B_xb = [Buf(f"xb{i}") for i in range(2)]
        B_st = [Buf(f"st{i}") for i in range(2)]
        B_xres = [Buf(f"xres{i}") for i in range(8)]
        B_S = [Buf(f"S{h}") for h in range(4)]
        B_Sb = [Buf(f"Sb{h}") for h in range(4)]
        B_Dtot = Buf("Dtot")
        B_const = Buf("const")
        B_alT = Buf("alT")
        B_halo = Buf("halo")
        B_xin = Buf("xin")
        B_gath = Buf("gath")

        def tmpbufs(prefix, n):
            return [Buf(f"{prefix}{i}") for i in range(n)]

        def aview(off_bytes, nbytes, dt=F32):
            a = arena[:, off_bytes // 4:(off_bytes + nbytes) // 4]
            return a if dt == F32 else a.bitcast(dt)

        xres = arena[:, :].rearrange("p (t d) -> p t d", t=8)
        cv_cc = [aview(0 + i * 2048, 2048) for i in range(2)]
        cv_uc = [aview(4096 + i * 2064, 2064) for i in range(2)]
        cv_tmp = [aview(8224 + i * 2048, 2048) for i in range(2)]
        cv_yc = [aview(12320 + i * 2048, 2048) for i in range(2)]
        cv_sq = [aview(16416 + i * 2048, 2048) for i in range(2)]
        cv_rr = [aview(20512 + i * 2048, 2048) for i in range(2)]
        uhT = aview(40960, 4096, BF16).rearrange("p (k t) -> p k t", k=KC)
        B_cc, B_uc, B_tmp, B_yc, B_sq, B_rr = (tmpbufs(n, 2) for n in ("cc", "uc", "tmp", "yc", "sq", "rr"))
        B_uhT = Buf("uhT")
        qT = aview(0, 4096, BF16).rearrange("p (h t) -> p h t", h=4)
        kd = aview(4096, 4096, BF16).rearrange("p (t c) -> p t c", t=4)
        vb = aview(8192, 8192, BF16).rearrange("p (t c) -> p t c", t=4)
        gate = aview(16384, 16384).rearrange("p (t c) -> p t c", t=4)
        g_e1 = [aview(32768 + i * 2048, 2048) for i in range(2)]
        g_sp = [aview(36864 + i * 2048, 2048) for i in range(2)]
        g_ex = [aview(40960 + i * 2048, 2048) for i in range(2)]
        g_og = [aview(45056 + i * 2048, 2048) for i in range(2)]
        g_ytm = [aview(49152 + i * 2048, 2048, BF16) for i in range(2)]
        g_dec = aview(53248, 128)
        B_qT = [Buf(f"qT{h}") for h in range(4)]
        B_kd = [Buf(f"kd{t}") for t in range(4)]
        B_vb = [Buf(f"vb{t}") for t in range(4)]
        B_gate = [Buf(f"gate{t}") for t in range(4)]
        B_e1, B_sp, B_ex, B_og, B_ytm = (tmpbufs(n, 2) for n in ("e1", "sp", "ex", "og", "ytm"))
        B_dec = [Buf(f"dec{t}") for t in range(4)]
        gath = aview(0, 8 * 1032 * 4).rearrange("p (r c) -> p r c", r=8)
        ftmp = [xb[0][:, i * 1024:(i + 1) * 1024].bitcast(F32) for i in range(2)]
        B_ftmp = tmpbufs("ftmp", 2)

        st_ss = [stats[:, 0:1], stats[:, 1:2]]
        st_rs = [stats[:, 2:3], stats[:, 3:4]]
        st_ss4 = [stats[:, 8:12], stats[:, 12:16]]
        st_rs4 = [stats[:, 16:20], stats[:, 20:24]]
        st_coef = stats[:, 24:28]
        st_hc = stats[:, 32:36]
        convg_s = stats[:, 40:48]
        B_st4 = tmpbufs("st4", 2)
        B_coef = Buf("coef")

        specs = []

        def rows_cols(w, r0, c0, ncols):
            return w[r0:r0 + 2048, c0:c0 + ncols].rearrange("(k p) c -> p k c", p=128)

        def spec_plain(tag, w, r0, c0, ncols=512):
            src = rows_cols(w, r0, c0, ncols)
            specs.append((tag, lambda e, dst, src=src, n=ncols: e.dma_start(out=dst[:, :, 0:n], in_=src), 1))

        def spec_conv(g):
            srcs = [rows_cols(w_in, 0, j * 1024 + g * 128, 128) for j in range(3)]
            specs.append((("conv", g), lambda e, dst, srcs=srcs: [e.dma_start(out=dst[:, :, j * 128:(j + 1) * 128], in_=srcs[j]) for j in range(3)], 3))

        def plan_gla(readout):
            if readout:
                spec_plain("q", w_in, 0, 3072)
            spec_plain("alow", w_in, 0, 6144, 16)
            spec_plain("k", w_in, 0, 3584)
            spec_plain("v0", w_in, 0, 4096)
            spec_plain("v1", w_in, 0, 4608)
            if readout:
                spec_plain("og0", w_in, 0, 5120)
                spec_plain("og1", w_in, 0, 5632)

        for hf in range(4):
            plan_gla(False)
        for t in range(2):
            for g in range(8):
                spec_conv(g)
            for hf in range(2):
                plan_gla(True)
            for cb in range(4):
                spec_plain(("wout", cb), w_out, 0, cb * 512)
            for fb in range(4):
                for fs in range(4):
                    spec_plain(("w1", fb, fs), w_ff1, 0, fb * 2048 + fs * 512)
                for cb in range(4):
                    spec_plain(("w2", fb, cb), w_ff2, fb * 2048, cb * 512)

        sl_state = {"issue": 0, "use": 0}

        def slab_issue(j):
            slot = j % NB
            tag, fn, nd = specs[j]
            P.emit("pool", lambda e, fn=fn, slot=slot: fn(e, slab[slot]), writes=[B_slab[slot]],
                   dma_sem=sem_slab[slot], ndma=nd)

        def slab_use(tag):
            i = sl_state["use"]
            assert specs[i][0] == tag, (specs[i][0], tag)
            while sl_state["issue"] < min(i + NB, len(specs)):
                slab_issue(sl_state["issue"])
                sl_state["issue"] += 1
            sl_state["use"] += 1
            return i % NB

        rot = {"big": 0}

        def mm_group(bank, out_ap, pairs, reads):
            n = len(pairs)
            op = None
            for i, (l, r) in enumerate(pairs):
                op = P.emit("pe", lambda e, l=l, r=r, i=i: e.matmul(out_ap, lhsT=l, rhs=r, start=(i == 0), stop=(i == n - 1)),
                            reads=reads, writes=[B_ps[bank]], signal=(i == n - 1))
            return op

        def norm_T(src_ap, B_src, gTt, dst_tok0, dst_tiles, pi, dstT=None, B_dst=None):
            dstT = uT if dstT is None else dstT
            i = pi % 2
            P.emit("act", lambda e: e.activation(out=xb[i][:], in_=src_ap, func=AF.Square, accum_out=st_ss[i]),
                   reads=[B_src], writes=[B_xb[i], B_st[i]])
            P.emit("act", lambda e: e.activation(out=st_rs[i], in_=st_ss[i], func=AF.Ln, scale=1.0 / D, bias=EPS),
                   reads=[B_st[i]], writes=[B_st[i]])
            P.emit("act", lambda e: e.activation(out=st_rs[i], in_=st_rs[i], func=AF.Exp, scale=-0.5),
                   reads=[B_st[i]], writes=[B_st[i]])
            P.emit("act", lambda e: e.activation(out=xb[i][:], in_=src_ap, func=AF.Copy, scale=st_rs[i]),
                   reads=[B_src, B_st[i]], writes=[B_xb[i]])
            banks = (0, 1) if i == 0 else (2, 3)
            for half in range(2):
                bk = banks[half]
                for j in range(8):
                    k = half * 8 + j
                    P.emit("pe", lambda e, k=k, j=j, bk=bk: e.transpose(out=psb[bk][:, j * 128:(j + 1) * 128],
                                                                         in_=xb[i][:, k * 128:(k + 1) * 128], identity=identb[:]),
                           reads=[B_xb[i], B_const], writes=[B_ps[bk]], signal=(j == 7))
                P.emit("dve", lambda e, half=half, bk=bk: e.tensor_tensor(
                    out=dstT[:, half * 8:(half + 1) * 8, dst_tok0:dst_tok0 + 128],
                    in0=psb[bk][:, 0:1024].rearrange("p (k t) -> p k t", k=8),
                    in1=gTt[:, half * 8:(half + 1) * 8].unsqueeze(2).to_broadcast([128, 8, 128]), op=ALU.mult),
                    reads=[B_ps[bk], B_const], writes=dst_tiles)

        def load_xt(row0, pi, src=None):
            i = pi % 2
            src = x if src is None else src
            P.emit("sp", lambda e: e.dma_start(out=xt[i][:], in_=src[row0:row0 + 128, :]), writes=[B_xt[i]], dma_sem=sem_xt[i])

        P.emit("pool", lambda e: e.memset(identf[:], 1.0), writes=[B_const])
        P.emit("pool", lambda e: e.affine_select(out=identf[:], in_=identf[:], pattern=[[-1, 128]], compare_op=ALU.is_equal,
                                                 fill=0.0, base=0, channel_multiplier=1), reads=[B_const], writes=[B_const])
        P.emit("pool", lambda e: e.memset(ones128[:], 1.0), reads=[B_const], writes=[B_const])
        P.emit("pool", lambda e: e.memset(Mrev[:], -1.0 / 16), reads=[B_const], writes=[B_const])
        P.emit("pool", lambda e: e.affine_select(out=Mrev[:], in_=Mrev[:], pattern=[[-1, 128]], compare_op=ALU.is_gt,
                                                 fill=0.0, base=0, channel_multiplier=1), reads=[B_const], writes=[B_const])
        P.emit("pool", lambda e: e.memset(Mrev[64:128, 0:64], 0.0), reads=[B_const], writes=[B_const])
        P.emit("pool", lambda e: e.memset(Ind[:], 0.0), reads=[B_const], writes=[B_const])
        P.emit("pool", lambda e: e.memset(Ind[0:64, 0:1], -1.0 / 16), reads=[B_const], writes=[B_const])
        P.emit("pool", lambda e: e.memset(Ind[64:128, 1:2], -1.0 / 16), reads=[B_const], writes=[B_const])
        P.emit("pool", lambda e: e.memset(alT[:], 1.0), writes=[B_alT])
        P.emit("pool", lambda e: e.affine_select(out=alT[:], in_=alT[:], pattern=[[0, 512]], compare_op=ALU.is_equal,
                                                 fill=0.0, base=-16, channel_multiplier=1), reads=[B_alT], writes=[B_alT])
        P.emit("pool", lambda e: e.memset(wg_ext[:], 0.0), reads=[B_const], writes=[B_const])
        P.emit("pool", lambda e: e.memset(Sx[:, 0:1024], 0.0), writes=B_S)
        P.emit("pool", lambda e: e.memset(Sx[:, 1024:1032], 1.0), writes=[B_Dtot])
        cl = []
        cl.append(P.emit("sp", lambda e: e.dma_start(out=wg_ext[0:16, :], in_=w_gate_up[:, :]), writes=[B_const], dma_sem=sem_c[0]))
        cl.append(P.emit("sp", lambda e: e.dma_start(out=wg_ext[16:17, :], in_=b_gate[:, :]), writes=[B_const], dma_sem=sem_c[0]))
        cl.append(P.emit("sp", lambda e: e.dma_start(out=g1T[:], in_=norm1_g.rearrange("(k p) -> p k", p=128), allow_slow_non_contiguous=True),
                         writes=[B_const], dma_sem=sem_c[0]))
        cl.append(P.emit("sp", lambda e: e.dma_start(out=g2T[:], in_=norm2_g.rearrange("(k p) -> p k", p=128), allow_slow_non_contiguous=True),
                         writes=[B_const], dma_sem=sem_c[0]))
        cl.append(P.emit("sp", lambda e: e.dma_start(out=convw[:], in_=conv_w.rearrange("(g p) j -> p g j", p=128)), writes=[B_const], dma_sem=sem_c[0]))
        cl.append(P.emit("sp", lambda e: e.dma_start(out=convg[:], in_=conv_norm_g.rearrange("(g p) -> p g", p=128), allow_slow_non_contiguous=True),
                         writes=[B_const], dma_sem=sem_c[0]))
        cl.append(P.emit("sp", lambda e: e.dma_start(out=cmask[:], in_=cmask_d[:, :]), writes=[B_const], dma_sem=sem_c[0]))
        cl.append(P.emit("sp", lambda e: e.dma_start(out=glag_rep[:], in_=gla_norm_g.broadcast_to([128, 256])), writes=[B_const], dma_sem=sem_c[0]))
        cl.append(P.emit("sp", lambda e: e.dma_start(out=gf_rep[:], in_=norm_f_g.broadcast_to([128, D])), writes=[B_const], dma_sem=sem_c[0]))
        P.emit("act", lambda e: e.activation(out=identb[:], in_=identf[:], func=AF.Copy), reads=[B_const], writes=[B_const])
        P.emit("act", lambda e: e.activation(out=convg_s, in_=convg[:], func=AF.Copy, scale=float(np.sqrt(128.0))), reads=[B_const], writes=[B_const])

        def gla_half(tok0, readout, yT_tok0=None):
            uts = [B_uT[(tok0 // 128) + i] for i in range(4)]
            if readout:
                s = slab_use("q")
                for h in range(4):
                    bk = rot["big"] % 4
                    rot["big"] += 1
                    mm_group(bk, ps[bk][:, :], [(slab[s][:, k, h * 128:(h + 1) * 128], uT[:, k, tok0:tok0 + 512]) for k in range(KC)],
                             reads=[B_slab[s]] + uts)
                    P.emit("act", lambda e, h=h, bk=bk: e.activation(out=qT[:, h, :], in_=ps[bk][:, :], func=AF.Copy, scale=float(128.0 ** -0.5)),
                           reads=[B_ps[bk]], writes=[B_qT[h]])
            s = slab_use("alow")
            bk = rot["big"] % 4
            rot["big"] += 1
            mm_group(bk, ps[bk][0:16, :], [(slab[s][:, k, 0:16], uT[:, k, tok0:tok0 + 512]) for k in range(KC)], reads=[B_slab[s]] + uts)
            P.emit("act", lambda e, bk=bk: e.activation(out=alT[0:16, :], in_=ps[bk][0:16, :], func=AF.Copy), reads=[B_ps[bk]], writes=[B_alT])
            s = slab_use("k")
            for tt in range(4):
                i2 = tt % 2
                bk = rot["big"] % 4
                rot["big"] += 1
                t0 = tok0 + tt * 128
                mm_group(bk, ps[bk][:, :], [(uT[:, k, t0:t0 + 128], slab[s][:, k, :]) for k in range(KC)], reads=[B_slab[s], uts[tt]])
                mm_group(4, ps[4][:, :], [(alT[:, tt * 128:(tt + 1) * 128], wg_ext[:, :])], reads=[B_alT, B_const])
                P.emit("act", lambda e, i2=i2: e.activation(out=g_e1[i2], in_=ps[4][:, :], func=AF.Exp, scale=-1.0), reads=[B_ps[4]], writes=[B_e1[i2]])
                P.emit("act", lambda e, i2=i2: e.activation(out=g_sp[i2], in_=g_e1[i2], func=AF.Ln, bias=1.0), reads=[B_e1[i2]], writes=[B_sp[i2]])
                mm_group(5, ps[5][:, :], [(Mrev[:, :], g_sp[i2])], reads=[B_const, B_sp[i2]])
                P.emit("act", lambda e, i2=i2: e.activation(out=g_ex[i2], in_=ps[5][:, :], func=AF.Exp), reads=[B_ps[5]], writes=[B_ex[i2]])
                P.emit("dve", lambda e, tt=tt, bk=bk, i2=i2: e.tensor_tensor(out=kd[:, tt, :], in0=ps[bk][:, :], in1=g_ex[i2], op=ALU.mult),
                       reads=[B_ps[bk], B_ex[i2]], writes=[B_kd[tt]])
                for h in range(4):
                    P.emit("pe", lambda e, h=h, i2=i2: e.matmul(ps[6][:, h * 2:h * 2 + 2], lhsT=g_sp[i2][:, h * 128:(h + 1) * 128], rhs=Ind[:, :], start=True, stop=True),
                           reads=[B_sp[i2], B_const], writes=[B_ps[6]], signal=(h == 3))
                P.emit("act", lambda e, tt=tt: e.activation(out=g_dec[:, tt * 8:(tt + 1) * 8], in_=ps[6][:, 0:8], func=AF.Exp), reads=[B_ps[6]], writes=[B_dec[tt]])
            for vs in range(2):
                s = slab_use(f"v{vs}")
                for tt in range(4):
                    bk = rot["big"] % 4
                    rot["big"] += 1
                    t0 = tok0 + tt * 128
                    mm_group(bk, ps[bk][:, :], [(uT[:, k, t0:t0 + 128], slab[s][:, k, :]) for k in range(KC)], reads=[B_slab[s], uts[tt]])
                    P.emit("act", lambda e, tt=tt, vs=vs, bk=bk: e.activation(out=vb[:, tt, vs * 512:(vs + 1) * 512], in_=ps[bk][:, :], func=AF.Copy),
                           reads=[B_ps[bk]], writes=[B_vb[tt]])
            if readout:
                for os_ in range(2):
                    s = slab_use(f"og{os_}")
                    for tt in range(4):
                        i2 = tt % 2
                        bk = rot["big"] % 4
                        rot["big"] += 1
                        t0 = tok0 + tt * 128
                        mm_group(bk, ps[bk][:, :], [(uT[:, k, t0:t0 + 128], slab[s][:, k, :]) for k in range(KC)], reads=[B_slab[s], uts[tt]])
                        P.emit("act", lambda e, bk=bk, i2=i2: e.activation(out=g_og[i2], in_=ps[bk][:, :], func=AF.Silu), reads=[B_ps[bk]], writes=[B_og[i2]])
                        P.emit("dve", lambda e, tt=tt, os_=os_, i2=i2: e.tensor_tensor(
                            out=gate[:, tt, os_ * 512:(os_ + 1) * 512].rearrange("p (h d) -> p h d", h=2),
                            in0=g_og[i2].rearrange("p (h d) -> p h d", h=2),
                            in1=glag_rep[:].unsqueeze(1).to_broadcast([128, 2, 256]), op=ALU.mult),
                            reads=[B_og[i2], B_const], writes=[B_gate[tt]])
            for tt in range(4):
                i2 = tt % 2
                for c in range(2):
                    for h in range(4):
                        mm_group(h, ps[h][:, 0:256], [(kd[64 * c:64 * c + 64, tt, h * 128:(h + 1) * 128], vb[64 * c:64 * c + 64, tt, h * 256:(h + 1) * 256])],
                                 reads=[B_kd[tt], B_vb[tt]])
                        di = tt * 8 + h * 2 + c
                        P.emit("dve", lambda e, h=h, di=di: e.scalar_tensor_tensor(out=Sx[:, h * 256:(h + 1) * 256], in0=Sx[:, h * 256:(h + 1) * 256],
                                                                                  scalar=g_dec[:, di:di + 1], in1=ps[h][:, 0:256], op0=ALU.mult, op1=ALU.add),
                               reads=[B_ps[h], B_dec[tt], B_S[h]], writes=[B_S[h]])
                        if readout:
                            P.emit("act", lambda e, h=h: e.activation(out=Sb_[:, h, :], in_=Sx[:, h * 256:(h + 1) * 256], func=AF.Copy),
                                   reads=[B_S[h]], writes=[B_Sb[h]])
                    if readout:
                        for h in range(4):
                            ob = 6 + h // 2
                            t0 = tt * 128 + 64 * c
                            P.emit("pe", lambda e, h=h, ob=ob, t0=t0, c=c: e.matmul(ps[ob][64 * c:64 * c + 64, (h % 2) * 256:(h % 2) * 256 + 256],
                                                                                 lhsT=qT[:, h, t0:t0 + 64], rhs=Sb_[:, h, :], start=True, stop=True),
                                   reads=[B_qT[h], B_Sb[h]], writes=[B_ps[ob]], signal=True)
                    else:
                        P.emit("dve", lambda e, tt=tt, c=c: e.tensor_tensor(
                            out=Sx[:, 1024:1028], in0=Sx[:, 1024:1028],
                            in1=g_dec[:, tt * 8:(tt + 1) * 8].rearrange("p (h c) -> p h c", c=2)[:, :, c], op=ALU.mult),
                            reads=[B_dec[tt], B_Dtot], writes=[B_Dtot])
                if readout:
                    for h in range(4):
                        ob = 6 + h // 2
                        P.emit("act", lambda e, h=h, ob=ob, i2=i2: e.activation(out=g_e1[i2][:, 0:256], in_=ps[ob][:, (h % 2) * 256:(h % 2) * 256 + 256],
                                                                                func=AF.Square, accum_out=st_ss4[i2][:, h:h + 1]),
                               reads=[B_ps[ob]], writes=[B_e1[i2], B_st4[i2]])
                    P.emit("act", lambda e, i2=i2: e.activation(out=st_rs4[i2], in_=st_ss4[i2], func=AF.Ln, scale=1.0 / 256, bias=EPS),
                           reads=[B_st4[i2]], writes=[B_st4[i2]])
                    P.emit("act", lambda e, i2=i2: e.activation(out=st_rs4[i2], in_=st_rs4[i2], func=AF.Exp, scale=-0.5),
                           reads=[B_st4[i2]], writes=[B_st4[i2]])
                    for h in range(4):
                        ob = 6 + h // 2
                        P.emit("dve", lambda e, h=h, ob=ob, i2=i2, tt=tt: e.scalar_tensor_tensor(
                            out=g_ytm[i2][:, h * 256:(h + 1) * 256], in0=ps[ob][:, (h % 2) * 256:(h % 2) * 256 + 256],
                            scalar=st_rs4[i2][:, h:h + 1], in1=gate[:, tt, h * 256:(h + 1) * 256], op0=ALU.mult, op1=ALU.mult),
                            reads=[B_ps[ob], B_st4[i2], B_gate[tt]], writes=[B_ytm[i2]])
                    tb = 4 + (tt % 2)
                    for j in range(8):
                        P.emit("pe", lambda e, j=j, tb=tb, i2=i2: e.transpose(out=psb[tb][:, j * 128:(j + 1) * 128],
                                                                             in_=g_ytm[i2][:, j * 128:(j + 1) * 128], identity=identb[:]),
                               reads=[B_ytm[i2], B_const], writes=[B_ps[tb]], signal=(j == 7))
                    y0 = yT_tok0 + tt * 128
                    P.emit("act", lambda e, tb=tb, y0=y0: e.activation(out=yT[:, 8:16, y0:y0 + 128],
                                                                       in_=psb[tb][:, 0:1024].rearrange("p (k t) -> p k t", k=8), func=AF.Copy),
                           reads=[B_ps[tb]], writes=B_yTc[8:16])

        pi = 0
        for hf in range(4):
            for tt in range(4):
                row0 = hf * 512 + tt * 128
                load_xt(row0, pi)
                norm_T(xt[pi % 2][:], B_xt[pi % 2], g1T, tt * 128, [B_uT[tt]], pi)
                pi += 1
            gla_half(0, False)
        P.barrier()
        P.emit("sp", lambda e: e.dma_start(out=xin[:, :], in_=Sx[:, :]), reads=B_S + [B_Dtot], writes=[B_xin], dma_sem=sem_c[1])
        E = P.eng["pool"]
        opx = B_xin.w
        E.prog.append(("wait", opx.sem, opx.val))
        E.prog.append(("raw", lambda e: e.collective_compute("AllGather", ALU.bypass, replica_groups=[list(range(NCORE))],
                                                             ins=[xin.ap().opt()], outs=[xout.ap().opt()]).then_inc(sem_cc, 1), None))
        ccop = Op(sem_cc, 1)
        P.emit("sp", lambda e: e.dma_start(out=gath, in_=xout.ap().rearrange("(r p) c -> p r c", p=128)), deps=[ccop], writes=[B_gath], dma_sem=sem_c[2])
        P.emit("dve", lambda e: e.memset(Sx[:, 0:1024], 0.0), reads=[B_xin], writes=B_S)
        for r in range(8):
            P.emit("dve", lambda e, r=r: e.tensor_scalar(out=st_coef, in0=gath[:, r, 1024:1028], scalar1=-1.0, scalar2=cmask[:, r:r + 1],
                                                         op0=ALU.add, op1=ALU.mult), reads=[B_gath, B_const], writes=[B_coef])
            P.emit("dve", lambda e: e.tensor_scalar(out=st_coef, in0=st_coef, scalar1=1.0, scalar2=None, op0=ALU.add), reads=[B_coef], writes=[B_coef])
            P.emit("dve", lambda e, r=r: e.tensor_scalar(out=gath[:, r, 0:1024], in0=gath[:, r, 0:1024], scalar1=cmask[:, r:r + 1], scalar2=None, op0=ALU.mult),
                   reads=[B_gath, B_const], writes=[B_gath])
            for h in range(4):
                P.emit("dve", lambda e, r=r, h=h: e.scalar_tensor_tensor(out=Sx[:, h * 256:(h + 1) * 256], in0=Sx[:, h * 256:(h + 1) * 256],
                                                                        scalar=st_coef[:, h:h + 1], in1=gath[:, r, h * 256:(h + 1) * 256],
                                                                        op0=ALU.mult, op1=ALU.add),
                       reads=[B_gath, B_coef, B_S[h]], writes=[B_S[h]])
        P.barrier()

        load_xt(0, pi, src=xh)
        norm_T(xt[pi % 2][:], B_xt[pi % 2], g1T, 0, [B_uhT], pi, dstT=uhT)
        pi += 1
        store_ops = []
        for t in range(2):
            P.barrier()
            for tt in range(8):
                load_xt(t * TT + tt * 128, pi)
                norm_T(xt[pi % 2][:], B_xt[pi % 2], g1T, tt * 128, [B_uT[tt]], pi)
                pi += 1
            P.barrier(extra=store_ops)
            for g in range(8):
                s = slab_use(("conv", g))
                if t == 0:
                    for j in (1, 2):
                        for k in range(KC):
                            P.emit("pe", lambda e, j=j, k=k, s=s: e.matmul(ps[6][:, (j - 1) * 2:(j - 1) * 2 + 2], lhsT=slab[s][:, k, j * 128:(j + 1) * 128],
                                                                       rhs=uhT[:, k, 126:128], start=(k == 0), stop=(k == KC - 1)),
                                   reads=[B_slab[s], B_uhT], writes=[B_ps[6]], signal=(k == KC - 1 and j == 2))
                    P.emit("act", lambda e: e.activation(out=st_hc[:, 0:2], in_=ps[6][:, 0:2], func=AF.Copy), reads=[B_ps[6]], writes=[B_coef])
                    P.emit("dve", lambda e, g=g: e.tensor_tensor(out=halo[:, g, :], in0=ps[6][:, 2:4], in1=st_hc[:, 0:2], op=ALU.mult),
                           reads=[B_ps[6], B_coef], writes=[B_halo])
                for nt in range(2):
                    i2 = nt
                    bks = (0, 1, 2) if nt == 0 else (3, 4, 5)
                    sbk = 6 + nt
                    uts = [B_uT[nt * 4 + i] for i in range(4)]
                    for j in range(3):
                        mm_group(bks[j], ps[bks[j]][:, :], [(slab[s][:, k, j * 128:(j + 1) * 128], uT[:, k, nt * 512:(nt + 1) * 512]) for k in range(KC)],
                                 reads=[B_slab[s]] + uts)
                    cbk, cck, chk = bks
                    P.emit("act", lambda e, i2=i2, cck=cck: e.activation(out=cv_cc[i2], in_=ps[cck][:, :], func=AF.Copy), reads=[B_ps[cck]], writes=[B_cc[i2]])
                    P.emit("act", lambda e, i2=i2, g=g: e.activation(out=cv_uc[i2][:, 0:2], in_=halo[:, g, :], func=AF.Copy), reads=[B_halo], writes=[B_uc[i2]])
                    P.emit("dve", lambda e, i2=i2, chk=chk: e.tensor_tensor(out=cv_uc[i2][:, 2:514], in0=ps[chk][:, :], in1=cv_cc[i2], op=ALU.mult),
                           reads=[B_ps[chk], B_cc[i2], B_uc[i2]], writes=[B_uc[i2]])
                    P.emit("act", lambda e, i2=i2, g=g: e.activation(out=halo[:, g, :], in_=cv_uc[i2][:, 512:514], func=AF.Copy), reads=[B_uc[i2]], writes=[B_halo])
                    P.emit("dve", lambda e, i2=i2, g=g: e.tensor_scalar(out=cv_tmp[i2], in0=cv_uc[i2][:, 2:514], scalar1=convw[:, g, 2:3], scalar2=None, op0=ALU.mult),
                           reads=[B_uc[i2], B_const], writes=[B_tmp[i2]])
                    P.emit("dve", lambda e, i2=i2, g=g: e.scalar_tensor_tensor(out=cv_tmp[i2], in0=cv_uc[i2][:, 1:513], scalar=convw[:, g, 1:2], in1=cv_tmp[i2],
                                                                              op0=ALU.mult, op1=ALU.add),
                           reads=[B_uc[i2], B_const, B_tmp[i2]], writes=[B_tmp[i2]])
                    P.emit("dve", lambda e, i2=i2, g=g: e.scalar_tensor_tensor(out=cv_tmp[i2], in0=cv_uc[i2][:, 0:512], scalar=convw[:, g, 0:1], in1=cv_tmp[i2],
                                                                              op0=ALU.mult, op1=ALU.add),
                           reads=[B_uc[i2], B_const, B_tmp[i2]], writes=[B_tmp[i2]])
                    P.emit("dve", lambda e, i2=i2, cbk=cbk: e.tensor_tensor(out=cv_yc[i2], in0=ps[cbk][:, :], in1=cv_tmp[i2], op=ALU.mult),
                           reads=[B_ps[cbk], B_tmp[i2]], writes=[B_yc[i2]])
                    P.emit("act", lambda e, i2=i2: e.activation(out=cv_sq[i2], in_=cv_yc[i2], func=AF.Square), reads=[B_yc[i2]], writes=[B_sq[i2]])
                    mm_group(sbk, ps[sbk][:, :], [(ones128[:, :], cv_sq[i2])], reads=[B_const, B_sq[i2]])
                    P.emit("act", lambda e, i2=i2, sbk=sbk: e.activation(out=cv_rr[i2], in_=ps[sbk][:, :], func=AF.Ln, scale=1.0, bias=128.0 * EPS),
                           reads=[B_ps[sbk]], writes=[B_rr[i2]])
                    P.emit("act", lambda e, i2=i2: e.activation(out=cv_rr[i2], in_=cv_rr[i2], func=AF.Exp, scale=-0.5), reads=[B_rr[i2]], writes=[B_rr[i2]])
                    P.emit("dve", lambda e, i2=i2, g=g, nt=nt: e.scalar_tensor_tensor(out=yT[:, g, nt * 512:(nt + 1) * 512], in0=cv_yc[i2], scalar=convg_s[:, g:g + 1],
                                                                                     in1=cv_rr[i2], op0=ALU.mult, op1=ALU.mult),
                           reads=[B_yc[i2], B_rr[i2], B_const], writes=[B_yTc[g]])
            P.barrier()
            for hf in range(2):
                gla_half(hf * 512, True, yT_tok0=hf * 512)
                P.barrier()
            for tt in range(8):
                r0 = t * TT + tt * 128
                P.emit("sp", lambda e, tt=tt, r0=r0: e.dma_start(out=xres[:, tt, :], in_=x[r0:r0 + 128, :]), writes=[B_xres[tt]], dma_sem=sem_xr[tt])
            for cb in range(4):
                s = slab_use(("wout", cb))
                for tt in range(8):
                    bk = rot["big"] % 8
                    rot["big"] += 1
                    mm_group(bk, ps[bk][:, :], [(yT[:, k, tt * 128:(tt + 1) * 128], slab[s][:, k, :]) for k in range(KC)], reads=[B_slab[s]] + B_yTc)
                    P.emit("dve", lambda e, tt=tt, cb=cb, bk=bk: e.tensor_tensor(out=xres[:, tt, cb * 512:(cb + 1) * 512], in0=xres[:, tt, cb * 512:(cb + 1) * 512],
                                                                               in1=ps[bk][:, :], op=ALU.add),
                           reads=[B_ps[bk], B_xres[tt]], writes=[B_xres[tt]])
            for tt in range(8):
                norm_T(xres[:, tt, :], B_xres[tt], g2T, tt * 128, [B_uT[tt]], pi)
                pi += 1
            P.barrier()
            for fb in range(4):
                for fs in range(4):
                    s = slab_use(("w1", fb, fs))
                    for fc in range(4):
                        j = fs * 4 + fc
                        for nt in range(2):
                            bk = rot["big"] % 8
                            rot["big"] += 1
                            i2 = rot["big"] % 2
                            mm_group(bk, ps[bk][:, :], [(slab[s][:, k, fc * 128:(fc + 1) * 128], uT[:, k, nt * 512:(nt + 1) * 512]) for k in range(KC)],
                                     reads=[B_slab[s]] + B_uT[nt * 4:(nt + 1) * 4])
                            P.emit("act", lambda e, bk=bk, i2=i2: e.activation(out=ftmp[i2], in_=ps[bk][:, :], func=AF.Relu), reads=[B_ps[bk]], writes=[B_ftmp[i2]])
                            P.emit("dve", lambda e, j=j, nt=nt, i2=i2: e.tensor_tensor(out=yT[:, j, nt * 512:(nt + 1) * 512], in0=ftmp[i2], in1=ftmp[i2], op=ALU.mult),
                                   reads=[B_ftmp[i2]], writes=[B_yTc[j]])
                for cb in range(4):
                    s = slab_use(("w2", fb, cb))
                    for tt in range(8):
                        bk = rot["big"] % 8
                        rot["big"] += 1
                        mm_group(bk, ps[bk][:, :], [(yT[:, k, tt * 128:(tt + 1) * 128], slab[s][:, k, :]) for k in range(KC)], reads=[B_slab[s]] + B_yTc)
                        P.emit("dve", lambda e, tt=tt, cb=cb, bk=bk: e.tensor_tensor(out=xres[:, tt, cb * 512:(cb + 1) * 512], in0=xres[:, tt, cb * 512:(cb + 1) * 512],
                                                                                   in1=ps[bk][:, :], op=ALU.add),
                               reads=[B_ps[bk], B_xres[tt]], writes=[B_xres[tt]])
            for tt in range(8):
                i = tt % 2
                P.emit("act", lambda e, tt=tt, i=i: e.activation(out=xb[1][:], in_=xres[:, tt, :], func=AF.Square, accum_out=st_ss[i]),
                       reads=[B_xres[tt]], writes=[B_xb[1], B_st[i]])
                P.emit("act", lambda e, i=i: e.activation(out=st_rs[i], in_=st_ss[i], func=AF.Ln, scale=1.0 / D, bias=EPS), reads=[B_st[i]], writes=[B_st[i]])
                P.emit("act", lambda e, i=i: e.activation(out=st_rs[i], in_=st_rs[i], func=AF.Exp, scale=-0.5), reads=[B_st[i]], writes=[B_st[i]])
                P.emit("dve", lambda e, tt=tt, i=i: e.scalar_tensor_tensor(out=xres[:, tt, :], in0=xres[:, tt, :], scalar=st_rs[i], in1=gf_rep[:],
                                                                          op0=ALU.mult, op1=ALU.mult),
                       reads=[B_xres[tt], B_st[i], B_const], writes=[B_xres[tt]])
                r0 = t * TT + tt * 128
                store_ops.append(P.emit("sp", lambda e, tt=tt, r0=r0: e.dma_start(out=out[r0:r0 + 128, :], in_=xres[:, tt, :]),
                                        reads=[B_xres[tt]], dma_sem=sem_st[i]))
        assert sl_state["use"] == len(specs)
        for i in range(2):
            k = id(sem_st[i])
            P.eng["sp"].prog.append(("wait", sem_st[i], P.dma_count[k]))

        for n, fn in (("pe", block.tensor), ("act", block.scalar), ("dve", block.vector), ("pool", block.gpsimd), ("sp", block.sync)):
            fn(lambda e, n=n: P.run_engine(n, e))
    return nc


_NC_CACHE = {}


def _prep_inputs(inputs):
    f = lambda a: np.ascontiguousarray(np.asarray(a, dtype=np.float32))
    x = f(inputs["x"])
    shared = {
        "norm1_g": f(inputs["norm1_g"]).reshape(D),
        "w_in": f(inputs["w_in"]).reshape(D, DIN),
        "w_gate_up": f(inputs["w_gate_up"]).reshape(16, 512),
        "b_gate": f(inputs["b_gate"]).reshape(1, 512),
        "conv_w": f(inputs["conv_w"]).reshape(1024, 3),
        "conv_norm_g": f(inputs["conv_norm_g"]).reshape(1024),
        "gla_norm_g": f(inputs["gla_norm_g"]).reshape(1, 256),
        "w_out": f(inputs["w_out"]).reshape(D, D),
        "norm2_g": f(inputs["norm2_g"]).reshape(D),
        "w_ff1": f(inputs["w_ff1"]).reshape(D, DFF),
        "w_ff2": f(inputs["w_ff2"]).reshape(DFF, D),
        "norm_f_g": f(inputs["norm_f_g"]).reshape(1, D),
    }
    in_maps = []
    for c in range(NCORE):
        b, q = c // 4, c % 4
        m = dict(shared)
        m["x"] = np.ascontiguousarray(x[b, q * TOK:(q + 1) * TOK, :])
        xhalo = np.zeros((128, D), np.float32)
        if q > 0:
            xhalo[126:128] = x[b, q * TOK - 2:q * TOK, :]
        m["xh"] = xhalo
        cm = np.zeros((128, 8), np.float32)
        for r in range(NCORE):
            if r // 4 == b and r < c:
                cm[:, r] = 1.0
        m["cmask"] = cm
        in_maps.append(m)
    return in_maps


def kernel(**inputs):
    if "nc" not in _NC_CACHE:
        _NC_CACHE["nc"] = build_program()
    nc = _NC_CACHE["nc"]
    in_maps = _prep_inputs(inputs)
    res = run_bass_kernel_spmd(nc, in_maps, core_ids=list(range(NCORE)))
    outp = np.empty((2, 8192, D), np.float32)
    for c in range(NCORE):
        b, q = c // 4, c % 4
        outp[b, q * TOK:(q + 1) * TOK, :] = np.asarray(res.results[c]["out"], dtype=np.float32).reshape(TOK, D)
    return outp
```

```python
import numpy as np
from contextlib import ExitStack
import concourse.bass as bass
import concourse.mybir as mybir
from concourse.bass_utils import run_bass_kernel_spmd

F32 = mybir.dt.float32
BF16 = mybir.dt.bfloat16
AF = mybir.ActivationFunctionType
ALU = mybir.AluOpType

NCORE = 8
D = 2048
DIN = 6160
DFF = 8192
TOK = 2048
TT = 1024
KC = 16
EPS = 1e-6
NB = 2
_NO_CC = False
import os as _os
_DBG = _os.environ.get("KDBG", "")


class Op:
    __slots__ = ("sem", "val")

    def __init__(self, sem, val):
        self.sem = sem
        self.val = val


class Buf:
    __slots__ = ("name", "w", "r", "excl")

    def __init__(self, name, excl=False):
        self.name = name
        self.w = None
        self.r = {}
        self.excl = excl


class _Eng:
    def __init__(self, name, sem):
        self.name = name
        self.sem = sem
        self.count = 0
        self.prog = []
        self.waited = {}


class Prog:
    ENGS = ("pe", "act", "dve", "pool", "sp")

    def __init__(self, sems):
        self.eng = {n: _Eng(n, sems[n]) for n in self.ENGS}
        self.dma_count = {}
        self.bar = {}

    @staticmethod
    def _need(need, op):
        if op is None:
            return
        k = id(op.sem)
        if k not in need or need[k].val < op.val:
            need[k] = op

    def emit(self, eng, fn, reads=(), writes=(), deps=(), signal=True, dma_sem=None, ndma=1):
        E = self.eng[eng]
        need = {}
        xr = [b for b in reads if b.excl]
        if xr:
            reads = [b for b in reads if not b.excl]
            writes = list(writes) + xr
        for b in reads:
            self._need(need, b.w)
        for b in writes:
            self._need(need, b.w)
            for r in b.r.values():
                self._need(need, r)
        for d in deps:
            self._need(need, d)
        for d in self.bar.values():
            self._need(need, d)
        for k, op in need.items():
            if eng == "pe" and op.sem is E.sem:
                continue
            if E.waited.get(k, 0) < op.val:
                E.waited[k] = op.val
                E.prog.append(("wait", op.sem, op.val))
        if dma_sem is not None:
            k = id(dma_sem)
            self.dma_count[k] = self.dma_count.get(k, 0) + 16 * ndma
            op = Op(dma_sem, self.dma_count[k])
            E.prog.append(("dma", fn, dma_sem))
        elif signal:
            E.count += 1
            op = Op(E.sem, E.count)
            E.prog.append(("sig", fn, E.sem))
        else:
            op = None
            E.prog.append(("raw", fn, None))
        if op is not None:
            k = id(op.sem)
            for b in reads:
                if k not in b.r or b.r[k].val < op.val:
                    b.r[k] = op
            for b in writes:
                b.w = op
                b.r = {}
        return op

    def barrier(self, extra=()):
        self.bar = {}
        for E in self.eng.values():
            if E.name in ("sp", "pool"):
                continue
            if E.count > 0:
                self._need(self.bar, Op(E.sem, E.count))
        for op in extra:
            self._need(self.bar, op)

    def run_engine(self, name, e):
        for item in self.eng[name].prog:
            kind = item[0]
            if kind == "wait":
                e.wait_ge(item[1], item[2])
            elif kind == "dma":
                r = item[1](e)
                if isinstance(r, (list, tuple)):
                    for ins in r:
                        ins.then_inc(item[2], 16)
                else:
                    r.then_inc(item[2], 16)
            elif kind == "sig":
                item[1](e).then_inc(item[2], 1)
            else:
                item[1](e)


def build_program():
    nc = bass.Bass("TRN2", target_bir_lowering=False)

    def din(name, shape):
        return nc.dram_tensor(name, shape, F32, kind="ExternalInput").ap()

    x = din("x", [TOK, D])
    xh = din("xh", [128, D])
    norm1_g = din("norm1_g", [D])
    w_in = din("w_in", [D, DIN])
    w_gate_up = din("w_gate_up", [16, 512])
    b_gate = din("b_gate", [1, 512])
    conv_w = din("conv_w", [1024, 3])
    conv_norm_g = din("conv_norm_g", [1024])
    gla_norm_g = din("gla_norm_g", [1, 256])
    w_out = din("w_out", [D, D])
    norm2_g = din("norm2_g", [D])
    w_ff1 = din("w_ff1", [D, DFF])
    w_ff2 = din("w_ff2", [DFF, D])
    norm_f_g = din("norm_f_g", [1, D])
    cmask_d = din("cmask", [128, 8])
    out = nc.dram_tensor("out", [TOK, D], F32, kind="ExternalOutput").ap()
    xin = nc.dram_tensor("xch_in", [128, 1032], F32)
    xout = nc.dram_tensor("xch_out", [128 * NCORE, 1032], F32)

    with ExitStack() as es:
        def sb(name, shape, dt=F32):
            return es.enter_context(nc.sbuf_tensor(name, shape, dt))

        slab = [sb(f"slab{i}", [128, KC, 512], BF16) for i in range(NB)]
        uT = sb("uT", [128, KC, TT], BF16)
        yT = sb("yT", [128, KC, TT], BF16)
        arena = sb("arena", [128, 16384], F32)
        xt = [sb(f"xt{i}", [128, D], F32) for i in range(2)]
        xb = [sb(f"xb{i}", [128, D], BF16) for i in range(2)]
        Sx = sb("Sx", [128, 1032], F32)
        Sb_ = sb("Sb", [128, 2, 4, 256], BF16)
        gf_rep = sb("gf_rep", [128, D], F32)
        glag_rep = sb("glag_rep", [128, 256], F32)
        wg_ext = sb("wg_ext", [128, 512], F32)
        alT = sb("alT", [128, 512], F32)
        identb = sb("identb", [128, 128], BF16)
        ones128 = sb("ones128", [128, 128], F32)
        Mrev = sb("Mrev", [128, 128], F32)
        Ind = sb("Ind", [128, 2], F32)
        g1T = sb("g1T", [128, KC], F32)
        g2T = sb("g2T", [128, KC], F32)
        convw = sb("convw", [128, 8, 3], F32)
        convg = sb("convg", [128, 8], F32)
        cmask = sb("cmask_sb", [128, 8], F32)
        halo = sb("halo", [128, 8, 2], F32)
        stats = sb("stats", [128, 64], F32)
        decs = sb("decs", [128, 64], F32)
        w_al = sb("w_al", [128, KC, 16], BF16)

        ps = [es.enter_context(nc.psum_tensor(f"ps{i}", [128, 512], F32)) for i in range(8)]
        psb = [p[:, :].bitcast(BF16) for p in ps]

        sems = {n: es.enter_context(nc.semaphore(f"s_{n}")) for n in Prog.ENGS}
        sem_slab = [es.enter_context(nc.semaphore(f"s_slab{i}")) for i in range(NB)]
        sem_xt = [es.enter_context(nc.semaphore(f"s_xt{i}")) for i in range(2)]
        sem_xr = [es.enter_context(nc.semaphore(f"s_xr{i}")) for i in range(8)]
        sem_st = [es.enter_context(nc.semaphore(f"s_st{i}")) for i in range(8)]
        sem_gb = [es.enter_context(nc.semaphore(f"s_gb{i}")) for i in range(2)]
        sem_c = [es.enter_context(nc.semaphore(f"s_c{i}")) for i in range(4)]
        sem_cc = es.enter_context(nc.semaphore("s_cc"))
        block = es.enter_context(nc.Block())

        P = Prog(sems)

        B_slab = [Buf(f"slab{i}") for i in range(NB)]
        B_uT = [Buf(f"uT{i}") for i in range(8)]
        B_yTc = [Buf(f"yTc{i}") for i in range(KC)]
        B_ps = [Buf(f"ps{i}", excl=True) for i in range(8)]
        B_xt = [Buf(f"xt{i}") for i in range(2)]
        B_xb = [Buf(f"xb{i}") for i in range(2)]
        B_st = [Buf(f"st{i}") for i in range(2)]
        B_xres = [Buf(f"xres{i}") for i in range(8)]
        B_S = [Buf(f"S{h}") for h in range(4)]
        B_Sb = [[Buf(f"Sb{i}_{h}") for h in range(4)] for i in range(2)]
        B_Dtot = Buf("Dtot")
        B_const = Buf("const")
        B_alT = Buf("alT")
        B_halo = Buf("halo")
        B_xin = Buf("xin")
        B_gath = Buf("gath")

        def tmpbufs(prefix, n):
            return [Buf(f"{prefix}{i}") for i in range(n)]

        def aview(off_bytes, nbytes, dt=F32):
            a = arena[:, off_bytes // 4:(off_bytes + nbytes) // 4]
            return a if dt == F32 else a.bitcast(dt)

        xres = arena[:, :].rearrange("p (t d) -> p t d", t=8)
        kd = aview(0, 8192, BF16).rearrange("p (t c) -> p t c", t=8)
        vb = aview(8192, 16384, BF16).rearrange("p (t c) -> p t c", t=8)
        qT = aview(24576, 8192, BF16).rearrange("p (h t) -> p h t", h=4)
        gate = aview(32768, 32768).rearrange("p (t c) -> p t c", t=8)
        CV = 24576
        cv_cc = [aview(CV + i * 2048, 2048) for i in range(2)]
        cv_uc = [aview(CV + 4096 + i * 2064, 2064) for i in range(2)]
        cv_tmp = [aview(CV + 8224 + i * 2048, 2048) for i in range(2)]
        cv_yc = [aview(CV + 12320 + i * 2048, 2048) for i in range(2)]
        cv_sq = [aview(CV + 16416 + i * 2048, 2048) for i in range(2)]
        cv_rr = [aview(CV + 20512 + i * 2048, 2048) for i in range(2)]
        gbuf = [aview(49408 + i * 4128, 4128) for i in range(2)]
        uhT = aview(59648, 4096, BF16).rearrange("p (k t) -> p k t", k=KC)
        identf = aview(49152, 512)
        grow = aview(49664, 1536).rearrange("p (j c) -> p j c", j=3)
        kd1 = aview(24576, 8192, BF16).rearrange("p (t c) -> p t c", t=8)
        vb1 = aview(32768, 16384, BF16).rearrange("p (t c) -> p t c", t=8)
        dec1 = aview(59392, 256)
        B_cc, B_uc, B_tmp, B_yc, B_sq, B_rr = (tmpbufs(n, 2) for n in ("cc", "uc", "tmp", "yc", "sq", "rr"))
        B_uhT = Buf("uhT")
        B_gb = tmpbufs("gb", 2)
        tmpG = {"e1": [aview(49152, 2048), aview(49152, 2048)],
                "sp": [aview(51200 + i * 2048, 2048) for i in range(2)],
                "ex": [aview(55296 + i * 2048, 2048) for i in range(2)]}
        tmpM = {"e1": [xt[0][:, i * 512:(i + 1) * 512] for i in range(2)],
                "sp": [xt[0][:, 1024 + i * 512:1024 + (i + 1) * 512] for i in range(2)],
                "ex": [xt[1][:, i * 512:(i + 1) * 512] for i in range(2)],
                "og": [xt[1][:, 1024 + i * 512:1024 + (i + 1) * 512] for i in range(2)],
                "ytm": [xb[1][:, i * 1024:(i + 1) * 1024] for i in range(2)]}
        B_qT = [Buf(f"qT{h}") for h in range(4)]
        B_kd = [Buf(f"kd{t}") for t in range(8)]
        B_vb = [Buf(f"vb{t}") for t in range(8)]
        B_gate = [Buf(f"gate{t}") for t in range(8)]
        B_e1m, B_sp, B_ex, B_og, B_ytm = (tmpbufs(n, 2) for n in ("e1", "sp", "ex", "og", "ytm"))
        B_e1g = Buf("e1g")
        B_dec = [Buf(f"dec{t}") for t in range(8)]
        B_kd1 = [Buf(f"kd1_{t}") for t in range(8)]
        B_vb1 = [Buf(f"vb1_{t}") for t in range(8)]
        B_dec1 = [Buf(f"dec1_{t}") for t in range(8)]
        kvset0 = (kd, vb, decs, B_kd, B_vb, B_dec)
        kvset1 = (kd1, vb1, dec1, B_kd1, B_vb1, B_dec1)
        ftmp = [xb[0][:, i * 1024:(i + 1) * 1024].bitcast(F32) for i in range(2)]
        B_ftmp = tmpbufs("ftmp", 2)

        st_ss = [stats[:, 0:1], stats[:, 1:2]]
        st_rs = [stats[:, 2:3], stats[:, 3:4]]
        st_ss4 = [stats[:, 8:12], stats[:, 12:16]]
        st_rs4 = [stats[:, 16:20], stats[:, 20:24]]
        st_coef = stats[:, 24:28]
        st_hc = stats[:, 32:36]
        convg_s = stats[:, 40:48]
        B_st4 = tmpbufs("st4", 2)
        B_coef = Buf("coef")

        specs = []

        def rows_cols(w, r0, c0, ncols):
            return w[r0:r0 + 2048, c0:c0 + ncols].rearrange("(k p) c -> p k c", p=128)

        def spec_plain(tag, w, r0, c0, ncols=512):
            src = rows_cols(w, r0, c0, ncols)
            specs.append((tag, lambda e, dst, src=src, n=ncols: e.dma_start(out=dst[:, :, 0:n], in_=src), 1))

        def spec_conv(g):
            srcs = [rows_cols(w_in, 0, j * 1024 + g * 128, 128) for j in range(3)]
            specs.append((("conv", g), lambda e, dst, srcs=srcs: [e.dma_start(out=dst[:, :, j * 128:(j + 1) * 128], in_=srcs[j]) for j in range(3)], 3))

        def plan_gla(readout, skip_kv=False):
            if readout:
                spec_plain("q", w_in, 0, 3072)
            if not skip_kv:
                spec_plain("k", w_in, 0, 3584)
                spec_plain("v0", w_in, 0, 4096)
                spec_plain("v1", w_in, 0, 4608)
            if readout:
                spec_plain("og0", w_in, 0, 5120)
                spec_plain("og1", w_in, 0, 5632)

        for sh in range(2):
            plan_gla(False)
        for t in range(2):
            for g in range(8):
                spec_conv(g)
            plan_gla(True, skip_kv=(t == 0))
            for cb in range(4):
                spec_plain(("wout", cb), w_out, 0, cb * 512)
            for fb in range(4):
                for fs in range(4):
                    spec_plain(("w1", fb, fs), w_ff1, 0, fb * 2048 + fs * 512)
                for cb in range(4):
                    spec_plain(("w2", fb, cb), w_ff2, fb * 2048, cb * 512)

        sl_state = {"issue": 0, "use": 0}

        def slab_issue(j):
            slot = j % NB
            tag, fn, nd = specs[j]
            P.emit("pool", lambda e, fn=fn, slot=slot: fn(e, slab[slot]), writes=[B_slab[slot]],
                   dma_sem=sem_slab[slot], ndma=nd)

        def slab_prefetch():
            i = sl_state["use"]
            while sl_state["issue"] < min(i + NB, len(specs)):
                slab_issue(sl_state["issue"])
                sl_state["issue"] += 1

        def slab_use(tag):
            i = sl_state["use"]
            assert specs[i][0] == tag, (specs[i][0], tag)
            slab_prefetch()
            sl_state["use"] += 1
            return i % NB

        rot = {"big": 0, "pi": 0}

        def mm_group(bank, out_ap, pairs, reads):
            n = len(pairs)
            op = None
            for i, (l, r) in enumerate(pairs):
                op = P.emit("pe", lambda e, l=l, r=r, i=i: e.matmul(out_ap, lhsT=l, rhs=r, start=(i == 0), stop=(i == n - 1)),
                            reads=reads, writes=[B_ps[bank]], signal=(i == n - 1))
            return op

        def norm_head(src_ap, B_src, i):
            al = B_ftmp if i == 0 else []
            P.emit("act", lambda e: e.activation(out=xb[i][:], in_=src_ap, func=AF.Square, accum_out=st_ss[i]),
                   reads=[B_src], writes=[B_xb[i], B_st[i]] + al)
            P.emit("act", lambda e: e.activation(out=st_rs[i], in_=st_ss[i], func=AF.Ln, scale=1.0 / D, bias=EPS),
                   reads=[B_st[i]], writes=[B_st[i]])
            P.emit("act", lambda e: e.activation(out=st_rs[i], in_=st_rs[i], func=AF.Exp, scale=-0.5),
                   reads=[B_st[i]], writes=[B_st[i]])
            P.emit("act", lambda e: e.activation(out=xb[i][:], in_=src_ap, func=AF.Copy, scale=st_rs[i]),
                   reads=[B_src, B_st[i]], writes=[B_xb[i]])

        def norm_tail(gTt, dstT, dst_tok0, dst_tiles, i):
            banks = (0, 1) if i == 0 else (2, 3)
            for half in range(2):
                bk = banks[half]
                for j in range(8):
                    k = half * 8 + j
                    P.emit("pe", lambda e, k=k, j=j, bk=bk: e.transpose(out=psb[bk][:, j * 128:(j + 1) * 128],
                                                                         in_=xb[i][:, k * 128:(k + 1) * 128], identity=identb[:]),
                           reads=[B_xb[i], B_const], writes=[B_ps[bk]], signal=(j == 7))
                P.emit("dve", lambda e, half=half, bk=bk: e.tensor_tensor(
                    out=dstT[:, half * 8:(half + 1) * 8, dst_tok0:dst_tok0 + 128],
                    in0=psb[bk][:, 0:1024].rearrange("p (k t) -> p k t", k=8),
                    in1=gTt[:, half * 8:(half + 1) * 8].unsqueeze(2).to_broadcast([128, 8, 128]), op=ALU.mult),
                    reads=[B_ps[bk], B_const], writes=dst_tiles)

        def x_tile_tasks(src_dram, row0, gTt, dstT, tok0, dst_tiles):
            st = {}

            def head():
                i = rot["pi"] % 2
                rot["pi"] += 1
                st["i"] = i
                P.emit("sp", lambda e: e.dma_start(out=xt[i][:], in_=src_dram[row0:row0 + 128, :]), writes=[B_xt[i]], dma_sem=sem_xt[i])
                norm_head(xt[i][:], B_xt[i], i)

            def tail():
                norm_tail(gTt, dstT, tok0, dst_tiles, st["i"])
            return head, tail

        def lagged(tasks):
            hs, ts = tasks[0::2], tasks[1::2]
            out_ = [hs[0]]
            for j in range(1, len(hs)):
                out_.append(hs[j])
                out_.append(ts[j - 1])
            out_.append(ts[-1])
            return out_

        def run_tasks(tasks):
            for f in tasks:
                f()

        class Interleaver:
            def __init__(self, tasks, nslots):
                self.tasks = list(tasks)
                self.n = nslots
                self.done = 0
                self.slot = 0

            def tick(self):
                self.slot += 1
                want = (len(self.tasks) * self.slot) // self.n if self.n > 0 else len(self.tasks)
                while self.done < min(want, len(self.tasks)):
                    self.tasks[self.done]()
                    self.done += 1

            def flush(self):
                while self.done < len(self.tasks):
                    self.tasks[self.done]()
                    self.done += 1

        first_tasks = []
        for tt in range(8):
            first_tasks.extend(x_tile_tasks(x, tt * 128, g1T, uT, tt * 128, [B_uT[tt]]))
        B_grow = Buf("grow")
        P.emit("pool", lambda e: e.memset(wg_ext[:], 0.0), writes=[B_const])
        P.emit("pool", lambda e: e.dma_start(out=w_al[:], in_=rows_cols(w_in, 0, 6144, 16)), writes=[B_const], dma_sem=sem_c[2])
        P.emit("act", lambda e: e.dma_start(out=grow[0:16, 0, :], in_=norm1_g.rearrange("(k p) -> k p", p=128)), writes=[B_grow], dma_sem=sem_c[0])
        P.emit("act", lambda e: e.dma_start(out=grow[0:16, 1, :], in_=norm2_g.rearrange("(k p) -> k p", p=128)), writes=[B_grow], dma_sem=sem_c[0])
        P.emit("act", lambda e: e.dma_start(out=grow[0:8, 2, :], in_=conv_norm_g.rearrange("(k p) -> k p", p=128)), writes=[B_grow], dma_sem=sem_c[0])
        P.emit("act", lambda e: e.dma_start(out=wg_ext[0:16, :], in_=w_gate_up[:, :]), writes=[B_const], dma_sem=sem_c[3])
        P.emit("act", lambda e: e.dma_start(out=wg_ext[16:17, :], in_=b_gate[:, :]), writes=[B_const], dma_sem=sem_c[3])
        P.emit("act", lambda e: e.dma_start(out=convw[:], in_=conv_w.rearrange("(g p) j -> p g j", p=128)), writes=[B_const], dma_sem=sem_c[3])
        P.emit("act", lambda e: e.dma_start(out=cmask[:], in_=cmask_d[:, :]), writes=[B_const], dma_sem=sem_c[3])
        P.emit("act", lambda e: e.dma_start(out=glag_rep[:], in_=gla_norm_g.broadcast_to([128, 256])), writes=[B_const], dma_sem=sem_c[3])
        P.emit("act", lambda e: e.dma_start(out=gf_rep[:], in_=norm_f_g.broadcast_to([128, D])), writes=[B_const], dma_sem=sem_c[3])
        first_tasks = lagged(first_tasks)
        first_tasks[0]()
        P.emit("pool", lambda e: e.memset(identf, 1.0), writes=[B_const])
        P.emit("pool", lambda e: e.affine_select(out=identf, in_=identf, pattern=[[-1, 128]], compare_op=ALU.is_equal,
                                                 fill=0.0, base=0, channel_multiplier=1), reads=[B_const], writes=[B_const])
        P.emit("pool", lambda e: e.memset(ones128[:], 1.0), reads=[B_const], writes=[B_const])
        P.emit("pool", lambda e: e.memset(Mrev[:], -1.0 / 16), reads=[B_const], writes=[B_const])
        P.emit("pool", lambda e: e.affine_select(out=Mrev[:], in_=Mrev[:], pattern=[[-1, 128]], compare_op=ALU.is_gt,
                                                 fill=0.0, base=0, channel_multiplier=1), reads=[B_const], writes=[B_const])
        P.emit("pool", lambda e: e.memset(Mrev[64:128, 0:64], 0.0), reads=[B_const], writes=[B_const])
        P.emit("pool", lambda e: e.memset(Ind[:], 0.0), reads=[B_const], writes=[B_const])
        P.emit("pool", lambda e: e.memset(Ind[0:64, 0:1], -1.0 / 16), reads=[B_const], writes=[B_const])
        P.emit("pool", lambda e: e.memset(Ind[64:128, 1:2], -1.0 / 16), reads=[B_const], writes=[B_const])
        P.emit("pool", lambda e: e.memset(alT[:], 1.0), writes=[B_alT])
        P.emit("pool", lambda e: e.affine_select(out=alT[:], in_=alT[:], pattern=[[0, 512]], compare_op=ALU.is_equal,
                                                 fill=0.0, base=-16, channel_multiplier=1), reads=[B_alT], writes=[B_alT])
        P.emit("pool", lambda e: e.memset(Sx[:, 0:1024], 0.0), writes=B_S)
        P.emit("pool", lambda e: e.memset(Sx[:, 1024:1032], 1.0), writes=[B_Dtot])
        P.emit("dve", lambda e: e.tensor_copy(out=identb[:], in_=identf), reads=[B_const], writes=[B_const])
        for j, (dst, n) in enumerate(((g1T, 16), (g2T, 16), (convg, 8))):
            P.emit("pe", lambda e, j=j, n=n: e.transpose(out=ps[7][:, j * 16:j * 16 + n], in_=grow[0:n, j, :], identity=identf[0:n, 0:n]),
                   reads=[B_grow, B_const], writes=[B_ps[7]])
            P.emit("dve", lambda e, j=j, n=n, dst=dst: e.tensor_copy(out=dst[:, 0:n], in_=ps[7][:, j * 16:j * 16 + n]), reads=[B_ps[7]], writes=[B_const])
        P.emit("dve", lambda e: e.tensor_scalar(out=convg_s, in0=convg[:], scalar1=float(np.sqrt(128.0)), scalar2=None, op0=ALU.mult),
               reads=[B_const], writes=[B_const])

        def gla(readout, tm, ilv, kvset=None, skip_kv=False):
            uts = B_uT
            kd, vb, decs, B_kd, B_vb, B_dec = kvset if kvset is not None else kvset0
            B_e1 = B_e1m if tm is tmpM else [B_e1g, B_e1g]
            if readout:
                s = slab_use("q")
                for h in range(4):
                    for nt in range(2):
                        bk = rot["big"] % 4
                        rot["big"] += 1
                        mm_group(bk, ps[bk][:, :], [(slab[s][:, k, h * 128:(h + 1) * 128], uT[:, k, nt * 512:(nt + 1) * 512]) for k in range(KC)],
                                 reads=[B_slab[s]] + uts[nt * 4:(nt + 1) * 4])
                        P.emit("act", lambda e, h=h, bk=bk, nt=nt: e.activation(out=qT[:, h, nt * 512:(nt + 1) * 512], in_=ps[bk][:, :], func=AF.Copy,
                                                                               scale=float(128.0 ** -0.5)),
                               reads=[B_ps[bk]], writes=[B_qT[h]])
            s_k = None if skip_kv else slab_use("k")

            def stage_a(tt):
                i2 = tt % 2
                if tt % 4 == 0:
                    nt = tt // 4
                    bk = rot["big"] % 4
                    rot["big"] += 1
                    mm_group(bk, ps[bk][0:16, :], [(w_al[:, k, :], uT[:, k, nt * 512:(nt + 1) * 512]) for k in range(KC)],
                             reads=[B_const] + uts[nt * 4:(nt + 1) * 4])
                    P.emit("act", lambda e, bk=bk: e.activation(out=alT[0:16, :], in_=ps[bk][0:16, :], func=AF.Copy), reads=[B_ps[bk]], writes=[B_alT])
                pb = 4 + i2
                c0 = (tt % 4) * 128
                mm_group(pb, ps[pb][:, :], [(alT[:, c0:c0 + 128], wg_ext[:, :])], reads=[B_alT, B_const])
                P.emit("act", lambda e, i2=i2, pb=pb: e.activation(out=tm["e1"][i2], in_=ps[pb][:, :], func=AF.Exp, scale=-1.0), reads=[B_ps[pb]], writes=[B_e1[i2]])
                P.emit("act", lambda e, i2=i2: e.activation(out=tm["sp"][i2], in_=tm["e1"][i2], func=AF.Ln, bias=1.0), reads=[B_e1[i2]], writes=[B_sp[i2]])

            if not skip_kv:
                stage_a(0)
            for tt in (range(8) if not skip_kv else ()):
                i2 = tt % 2
                bk = rot["big"] % 4
                rot["big"] += 1
                t0 = tt * 128
                mm_group(bk, ps[bk][:, :], [(uT[:, k, t0:t0 + 128], slab[s_k][:, k, :]) for k in range(KC)], reads=[B_slab[s_k], uts[tt]])
                if tt + 1 < 8:
                    stage_a(tt + 1)
                mm_group(6, ps[6][:, :], [(Mrev[:, :], tm["sp"][i2])], reads=[B_const, B_sp[i2]])
                P.emit("act", lambda e, i2=i2: e.activation(out=tm["ex"][i2], in_=ps[6][:, :], func=AF.Exp), reads=[B_ps[6]], writes=[B_ex[i2]])
                P.emit("dve", lambda e, tt=tt, bk=bk, i2=i2: e.tensor_tensor(out=kd[:, tt, :], in0=ps[bk][:, :], in1=tm["ex"][i2], op=ALU.mult),
                       reads=[B_ps[bk], B_ex[i2]], writes=[B_kd[tt]])
                for h in range(4):
                    P.emit("pe", lambda e, h=h, i2=i2: e.matmul(ps[7][:, h * 2:h * 2 + 2], lhsT=tm["sp"][i2][:, h * 128:(h + 1) * 128], rhs=Ind[:, :], start=True, stop=True),
                           reads=[B_sp[i2], B_const], writes=[B_ps[7]], signal=(h == 3))
                P.emit("act", lambda e, tt=tt: e.activation(out=decs[:, tt * 8:(tt + 1) * 8], in_=ps[7][:, 0:8], func=AF.Exp), reads=[B_ps[7]], writes=[B_dec[tt]])
            for vs in (range(2) if not skip_kv else ()):
                s = slab_use(f"v{vs}")
                for tt in range(8):
                    bk = rot["big"] % 4
                    rot["big"] += 1
                    t0 = tt * 128
                    mm_group(bk, ps[bk][:, :], [(uT[:, k, t0:t0 + 128], slab[s][:, k, :]) for k in range(KC)], reads=[B_slab[s], uts[tt]])
                    P.emit("act", lambda e, tt=tt, vs=vs, bk=bk: e.activation(out=vb[:, tt, vs * 512:(vs + 1) * 512], in_=ps[bk][:, :], func=AF.Copy),
                           reads=[B_ps[bk]], writes=[B_vb[tt]])
            if readout:
                for os_ in range(2):
                    s = slab_use(f"og{os_}")
                    for tt in range(8):
                        i2 = tt % 2
                        bk = rot["big"] % 4
                        rot["big"] += 1
                        t0 = tt * 128
                        mm_group(bk, ps[bk][:, :], [(uT[:, k, t0:t0 + 128], slab[s][:, k, :]) for k in range(KC)], reads=[B_slab[s], uts[tt]])
                        P.emit("act", lambda e, bk=bk, i2=i2: e.activation(out=tm["og"][i2], in_=ps[bk][:, :], func=AF.Silu), reads=[B_ps[bk]], writes=[B_og[i2]])
                        P.emit("dve", lambda e, tt=tt, os_=os_, i2=i2: e.tensor_tensor(
                            out=gate[:, tt, os_ * 512:(os_ + 1) * 512].rearrange("p (h d) -> p h d", h=2),
                            in0=tm["og"][i2].rearrange("p (h d) -> p h d", h=2),
                            in1=glag_rep[:].unsqueeze(1).to_broadcast([128, 2, 256]), op=ALU.mult),
                            reads=[B_og[i2], B_const], writes=[B_gate[tt]])
            slab_prefetch()

            def kv_update(ci):
                tt, c = ci // 2, ci % 2
                kb = (ci % 2) * 2
                for h in range(4):
                    bk = kb + h // 2
                    col = (h % 2) * 256
                    mm_group(bk, ps[bk][:, col:col + 256],
                             [(kd[64 * c:64 * c + 64, tt, h * 128:(h + 1) * 128], vb[64 * c:64 * c + 64, tt, h * 256:(h + 1) * 256])],
                             reads=[B_kd[tt], B_vb[tt]])
                for h in range(4):
                    bk = kb + h // 2
                    col = (h % 2) * 256
                    di = tt * 8 + h * 2 + c
                    P.emit("dve", lambda e, h=h, di=di, bk=bk, col=col: e.scalar_tensor_tensor(
                        out=Sx[:, h * 256:(h + 1) * 256], in0=Sx[:, h * 256:(h + 1) * 256],
                        scalar=decs[:, di:di + 1], in1=ps[bk][:, col:col + 256], op0=ALU.mult, op1=ALU.add),
                        reads=[B_ps[bk], B_dec[tt], B_S[h]], writes=[B_S[h]])
                    if readout:
                        sbi = ci % 2
                        P.emit("act", lambda e, h=h, sbi=sbi: e.activation(out=Sb_[:, sbi, h, :], in_=Sx[:, h * 256:(h + 1) * 256], func=AF.Copy),
                               reads=[B_S[h]], writes=[B_Sb[sbi][h]])
                if not readout:
                    P.emit("dve", lambda e, tt=tt, c=c: e.tensor_tensor(
                        out=Sx[:, 1024:1028], in0=Sx[:, 1024:1028],
                        in1=decs[:, tt * 8:(tt + 1) * 8].rearrange("p (h c) -> p h c", c=2)[:, :, c], op=ALU.mult),
                        reads=[B_dec[tt], B_Dtot], writes=[B_Dtot])

            def read_out(ci):
                tt, c = ci // 2, ci % 2
                sbi = ci % 2
                for h in range(4):
                    ob = 6 + h // 2
                    t0 = tt * 128 + 64 * c
                    P.emit("pe", lambda e, h=h, ob=ob, t0=t0, c=c, sbi=sbi: e.matmul(ps[ob][64 * c:64 * c + 64, (h % 2) * 256:(h % 2) * 256 + 256],
                                                                                  lhsT=qT[:, h, t0:t0 + 64], rhs=Sb_[:, sbi, h, :], start=True, stop=True),
                           reads=[B_qT[h], B_Sb[sbi][h]], writes=[B_ps[ob]], signal=True)

            def tile_head(tt):
                i2 = tt % 2
                for h in range(4):
                    ob = 6 + h // 2
                    P.emit("act", lambda e, h=h, ob=ob, i2=i2: e.activation(out=tm["e1"][i2][:, 0:256], in_=ps[ob][:, (h % 2) * 256:(h % 2) * 256 + 256],
                                                                            func=AF.Square, accum_out=st_ss4[i2][:, h:h + 1]),
                           reads=[B_ps[ob]], writes=[B_e1[i2], B_st4[i2]])
                P.emit("act", lambda e, i2=i2: e.activation(out=st_rs4[i2], in_=st_ss4[i2], func=AF.Ln, scale=1.0 / 256, bias=EPS),
                       reads=[B_st4[i2]], writes=[B_st4[i2]])
                P.emit("act", lambda e, i2=i2: e.activation(out=st_rs4[i2], in_=st_rs4[i2], func=AF.Exp, scale=-0.5),
                       reads=[B_st4[i2]], writes=[B_st4[i2]])
                for h in range(4):
                    ob = 6 + h // 2
                    P.emit("dve", lambda e, h=h, ob=ob, i2=i2, tt=tt: e.scalar_tensor_tensor(
                        out=tm["ytm"][i2][:, h * 256:(h + 1) * 256], in0=ps[ob][:, (h % 2) * 256:(h % 2) * 256 + 256],
                        scalar=st_rs4[i2][:, h:h + 1], in1=gate[:, tt, h * 256:(h + 1) * 256], op0=ALU.mult, op1=ALU.mult),
                        reads=[B_ps[ob], B_st4[i2], B_gate[tt]], writes=[B_ytm[i2]])

            def tile_tail(tt):
                i2 = tt % 2
                tb = 4 + (tt % 2)
                for j in range(8):
                    P.emit("pe", lambda e, j=j, tb=tb, i2=i2: e.transpose(out=psb[tb][:, j * 128:(j + 1) * 128],
                                                                         in_=tm["ytm"][i2][:, j * 128:(j + 1) * 128], identity=identb[:]),
                           reads=[B_ytm[i2], B_const], writes=[B_ps[tb]], signal=(j == 7))
                y0 = tt * 128
                P.emit("act", lambda e, tb=tb, y0=y0: e.activation(out=yT[:, 8:16, y0:y0 + 128],
                                                                   in_=psb[tb][:, 0:1024].rearrange("p (k t) -> p k t", k=8), func=AF.Copy),
                       reads=[B_ps[tb]], writes=B_yTc[8:16])

            kv_update(0)
            pend_tail = None
            for ci in range(16):
                if ci + 1 < 16 and "s" not in _DBG:
                    kv_update(ci + 1)
                if readout:
                    read_out(ci)
                    if pend_tail is not None:
                        tile_tail(pend_tail)
                        pend_tail = None
                    if ci % 2 == 1:
                        tile_head(ci // 2)
                        pend_tail = ci // 2
                if ci + 1 < 16 and "s" in _DBG:
                    kv_update(ci + 1)
                if ilv is not None:
                    ilv.tick()
            if readout and pend_tail is not None:
                tile_tail(pend_tail)
            if ilv is not None:
                ilv.flush()

        run_tasks(first_tasks[1:])
        for sh in range(2):
            if sh == 0:
                nxt = []
                for tt in range(8):
                    nxt.extend(x_tile_tasks(x, TT + tt * 128, g1T, uT, tt * 128, [B_uT[tt]]))
            else:
                nxt = list(x_tile_tasks(xh, 0, g1T, uhT, 0, [B_uhT]))
                for tt in range(8):
                    nxt.extend(x_tile_tasks(x, tt * 128, g1T, uT, tt * 128, [B_uT[tt]]))
            gla(False, tmpG, Interleaver(lagged(nxt), 16), kvset=(kvset0 if sh == 0 else kvset1))
        P.barrier()
        P.emit("sp", lambda e: e.dma_start(out=xin[:, :], in_=Sx[:, :]), reads=B_S + [B_Dtot], writes=[B_xin], dma_sem=sem_c[1])
        E = P.eng["pool"]
        opx = B_xin.w
        E.prog.append(("wait", opx.sem, opx.val))
        if _NO_CC:
            for r in range(8):
                E.prog.append(("raw", lambda e, r=r: e.dma_start(out=xout.ap()[r * 128:(r + 1) * 128, :], in_=xin.ap()[:, :]).then_inc(sem_cc, 16), None))
        else:
            E.prog.append(("raw", lambda e: e.collective_compute("AllGather", ALU.bypass, replica_groups=[list(range(NCORE))],
                                                                 ins=[xin.ap().opt()], outs=[xout.ap().opt()]).then_inc(sem_cc, 1), None))
        ccop = Op(sem_cc, 128 if _NO_CC else 1)

        def fold_state():
            P.emit("dve", lambda e: e.memset(Sx[:, 0:1024], 0.0), reads=[B_xin], writes=B_S)
            for r in range(8):
                gi = r % 2
                P.emit("sp", lambda e, r=r, gi=gi: e.dma_start(out=gbuf[gi], in_=xout.ap()[r * 128:(r + 1) * 128, :]), deps=[ccop], writes=[B_gb[gi]], dma_sem=sem_gb[gi])
                P.emit("dve", lambda e, r=r, gi=gi: e.tensor_scalar(out=st_coef, in0=gbuf[gi][:, 1024:1028], scalar1=-1.0, scalar2=cmask[:, r:r + 1],
                                                                    op0=ALU.add, op1=ALU.mult), reads=[B_gb[gi], B_const], writes=[B_coef])
                P.emit("dve", lambda e: e.tensor_scalar(out=st_coef, in0=st_coef, scalar1=1.0, scalar2=None, op0=ALU.add), reads=[B_coef], writes=[B_coef])
                P.emit("dve", lambda e, r=r, gi=gi: e.tensor_scalar(out=gbuf[gi][:, 0:1024], in0=gbuf[gi][:, 0:1024], scalar1=cmask[:, r:r + 1], scalar2=None, op0=ALU.mult),
                       reads=[B_gb[gi], B_const], writes=[B_gb[gi]])
                for h in range(4):
                    P.emit("dve", lambda e, gi=gi, h=h: e.scalar_tensor_tensor(out=Sx[:, h * 256:(h + 1) * 256], in0=Sx[:, h * 256:(h + 1) * 256],
                                                                             scalar=st_coef[:, h:h + 1], in1=gbuf[gi][:, h * 256:(h + 1) * 256],
                                                                             op0=ALU.mult, op1=ALU.add),
                           reads=[B_gb[gi], B_coef, B_S[h]], writes=[B_S[h]])

        store_ops = []
        for t in range(2):
            P.barrier(extra=store_ops)
            pend = None
            for g in range(8):
                if t == 0 and g == 4:
                    fold_state()
                s = slab_use(("conv", g))
                if t == 0:
                    for j in (1, 2):
                        for k in range(KC):
                            P.emit("pe", lambda e, j=j, k=k, s=s: e.matmul(ps[6][:, (j - 1) * 2:(j - 1) * 2 + 2], lhsT=slab[s][:, k, j * 128:(j + 1) * 128],
                                                                            rhs=uhT[:, k, 126:128], start=(k == 0), stop=(k == KC - 1)),
                                   reads=[B_slab[s], B_uhT], writes=[B_ps[6]], signal=(k == KC - 1 and j == 2))
                    P.emit("act", lambda e: e.activation(out=st_hc[:, 0:2], in_=ps[6][:, 0:2], func=AF.Copy), reads=[B_ps[6]], writes=[B_coef])
                    P.emit("dve", lambda e, g=g: e.tensor_tensor(out=halo[:, g, :], in0=ps[6][:, 2:4], in1=st_hc[:, 0:2], op=ALU.mult),
                           reads=[B_ps[6], B_coef], writes=[B_halo])
                for nt in range(2):
                    i2 = nt
                    bks = (0, 1, 2) if nt == 0 else (3, 4, 5)
                    uts = [B_uT[nt * 4 + i] for i in range(4)]
                    for j in range(3):
                        mm_group(bks[j], ps[bks[j]][:, :], [(slab[s][:, k, j * 128:(j + 1) * 128], uT[:, k, nt * 512:(nt + 1) * 512]) for k in range(KC)],
                                 reads=[B_slab[s]] + uts)
                    if pend is not None:
                        pend()
                    cbk, cck, chk = bks
                    P.emit("act", lambda e, i2=i2, cck=cck: e.activation(out=cv_cc[i2], in_=ps[cck][:, :], func=AF.Copy), reads=[B_ps[cck]], writes=[B_cc[i2]])
                    P.emit("act", lambda e, i2=i2, g=g: e.activation(out=cv_uc[i2][:, 0:2], in_=halo[:, g, :], func=AF.Copy), reads=[B_halo], writes=[B_uc[i2]])
                    P.emit("dve", lambda e, i2=i2, chk=chk: e.tensor_tensor(out=cv_uc[i2][:, 2:514], in0=ps[chk][:, :], in1=cv_cc[i2], op=ALU.mult),
                           reads=[B_ps[chk], B_cc[i2], B_uc[i2]], writes=[B_uc[i2]])
                    P.emit("act", lambda e, i2=i2, g=g: e.activation(out=halo[:, g, :], in_=cv_uc[i2][:, 512:514], func=AF.Copy), reads=[B_uc[i2]], writes=[B_halo])
                    P.emit("dve", lambda e, i2=i2, g=g: e.tensor_scalar(out=cv_tmp[i2], in0=cv_uc[i2][:, 2:514], scalar1=convw[:, g, 2:3], scalar2=None, op0=ALU.mult),
                           reads=[B_uc[i2], B_const], writes=[B_tmp[i2]])
                    P.emit("dve", lambda e, i2=i2, g=g: e.scalar_tensor_tensor(out=cv_tmp[i2], in0=cv_uc[i2][:, 1:513], scalar=convw[:, g, 1:2], in1=cv_tmp[i2],
                                                                              op0=ALU.mult, op1=ALU.add),
                           reads=[B_uc[i2], B_const, B_tmp[i2]], writes=[B_tmp[i2]])
                    P.emit("dve", lambda e, i2=i2, g=g: e.scalar_tensor_tensor(out=cv_tmp[i2], in0=cv_uc[i2][:, 0:512], scalar=convw[:, g, 0:1], in1=cv_tmp[i2],
                                                                              op0=ALU.mult, op1=ALU.add),
                           reads=[B_uc[i2], B_const, B_tmp[i2]], writes=[B_tmp[i2]])
                    P.emit("dve", lambda e, i2=i2, cbk=cbk: e.tensor_tensor(out=cv_yc[i2], in0=ps[cbk][:, :], in1=cv_tmp[i2], op=ALU.mult),
                           reads=[B_ps[cbk], B_tmp[i2]], writes=[B_yc[i2]])
                    P.emit("act", lambda e, i2=i2: e.activation(out=cv_sq[i2], in_=cv_yc[i2], func=AF.Square), reads=[B_yc[i2]], writes=[B_sq[i2]])

                    def conv_tail(i2=i2, g=g, nt=nt):
                        sbk = 6 + i2
                        mm_group(sbk, ps[sbk][:, :], [(ones128[:, :], cv_sq[i2])], reads=[B_const, B_sq[i2]])
                        P.emit("act", lambda e: e.activation(out=cv_rr[i2], in_=ps[sbk][:, :], func=AF.Ln, scale=1.0, bias=128.0 * EPS),
                               reads=[B_ps[sbk]], writes=[B_rr[i2]])
                        P.emit("act", lambda e: e.activation(out=cv_rr[i2], in_=cv_rr[i2], func=AF.Exp, scale=-0.5), reads=[B_rr[i2]], writes=[B_rr[i2]])
                        P.emit("dve", lambda e: e.scalar_tensor_tensor(out=yT[:, g, nt * 512:(nt + 1) * 512], in0=cv_yc[i2], scalar=convg_s[:, g:g + 1],
                                                                       in1=cv_rr[i2], op0=ALU.mult, op1=ALU.mult),
                               reads=[B_yc[i2], B_rr[i2], B_const], writes=[B_yTc[g]])
                    if "c" in _DBG:
                        conv_tail()
                        pend = None
                    else:
                        pend = conv_tail
            if pend is not None:
                pend()
            P.barrier()
            gla(True, tmpM, None, skip_kv=(t == 0))
            P.barrier()
            for tt in range(8):
                r0 = t * TT + tt * 128
                P.emit("sp", lambda e, tt=tt, r0=r0: e.dma_start(out=xres[:, tt, :], in_=x[r0:r0 + 128, :]), writes=[B_xres[tt]], dma_sem=sem_xr[tt])
            for cb in range(4):
                s = slab_use(("wout", cb))
                for tt in range(8):
                    bk = rot["big"] % 8
                    rot["big"] += 1
                    mm_group(bk, ps[bk][:, :], [(yT[:, k, tt * 128:(tt + 1) * 128], slab[s][:, k, :]) for k in range(KC)], reads=[B_slab[s]] + B_yTc)
                    P.emit("dve", lambda e, tt=tt, cb=cb, bk=bk: e.tensor_tensor(out=xres[:, tt, cb * 512:(cb + 1) * 512], in0=xres[:, tt, cb * 512:(cb + 1) * 512],
                                                                               in1=ps[bk][:, :], op=ALU.add),
                           reads=[B_ps[bk], B_xres[tt]], writes=[B_xres[tt]])
            prev = None
            for tt in range(8):
                i = rot["pi"] % 2
                rot["pi"] += 1
                norm_head(xres[:, tt, :], B_xres[tt], i)
                if prev is not None:
                    norm_tail(g2T, uT, prev[0] * 128, [B_uT[prev[0]]], prev[1])
                prev = (tt, i)
            norm_tail(g2T, uT, prev[0] * 128, [B_uT[prev[0]]], prev[1])
            ilv = None
            for fb in range(4):
                for fs in range(4):
                    s = slab_use(("w1", fb, fs))
                    for fc in range(4):
                        j = fs * 4 + fc
                        for nt in range(2):
                            bk = rot["big"] % 8
                            rot["big"] += 1
                            i2 = rot["big"] % 2
                            mm_group(bk, ps[bk][:, :], [(slab[s][:, k, fc * 128:(fc + 1) * 128], uT[:, k, nt * 512:(nt + 1) * 512]) for k in range(KC)],
                                     reads=[B_slab[s]] + B_uT[nt * 4:(nt + 1) * 4])
                            P.emit("act", lambda e, bk=bk, i2=i2: e.activation(out=ftmp[i2], in_=ps[bk][:, :], func=AF.Relu), reads=[B_ps[bk]],
                                   writes=[B_ftmp[i2], B_xb[0]])
                            P.emit("dve", lambda e, j=j, nt=nt, i2=i2: e.tensor_tensor(out=yT[:, j, nt * 512:(nt + 1) * 512], in0=ftmp[i2], in1=ftmp[i2], op=ALU.mult),
                                   reads=[B_ftmp[i2]], writes=[B_yTc[j]])
                if fb == 3 and t == 0:
                    nxt = []
                    for tt in range(8):
                        nxt.extend(x_tile_tasks(x, TT + tt * 128, g1T, uT, tt * 128, [B_uT[tt]]))
                    ilv = Interleaver(lagged(nxt), 32)
                for cb in range(4):
                    s = slab_use(("w2", fb, cb))
                    for tt in range(8):
                        bk = rot["big"] % 8
                        rot["big"] += 1
                        mm_group(bk, ps[bk][:, :], [(yT[:, k, tt * 128:(tt + 1) * 128], slab[s][:, k, :]) for k in range(KC)], reads=[B_slab[s]] + B_yTc)
                        P.emit("dve", lambda e, tt=tt, cb=cb, bk=bk: e.tensor_tensor(out=xres[:, tt, cb * 512:(cb + 1) * 512], in0=xres[:, tt, cb * 512:(cb + 1) * 512],
                                                                                   in1=ps[bk][:, :], op=ALU.add),
                               reads=[B_ps[bk], B_xres[tt]], writes=[B_xres[tt]])
                        if ilv is not None:
                            ilv.tick()
            if ilv is not None:
                ilv.flush()
            for tt in range(8):
                i = tt % 2
                P.emit("act", lambda e, tt=tt, i=i: e.activation(out=xb[1][:], in_=xres[:, tt, :], func=AF.Square, accum_out=st_ss[i]),
                       reads=[B_xres[tt]], writes=[B_xb[1], B_st[i]])
                P.emit("act", lambda e, i=i: e.activation(out=st_rs[i], in_=st_ss[i], func=AF.Ln, scale=1.0 / D, bias=EPS), reads=[B_st[i]], writes=[B_st[i]])
                P.emit("act", lambda e, i=i: e.activation(out=st_rs[i], in_=st_rs[i], func=AF.Exp, scale=-0.5), reads=[B_st[i]], writes=[B_st[i]])
                P.emit("dve", lambda e, tt=tt, i=i: e.scalar_tensor_tensor(out=xres[:, tt, :], in0=xres[:, tt, :], scalar=st_rs[i], in1=gf_rep[:],
                                                                          op0=ALU.mult, op1=ALU.mult),
                       reads=[B_xres[tt], B_st[i], B_const], writes=[B_xres[tt]])
                r0 = t * TT + tt * 128
                store_ops.append(P.emit("sp", lambda e, tt=tt, r0=r0: e.dma_start(out=out[r0:r0 + 128, :], in_=xres[:, tt, :]),
                                        reads=[B_xres[tt]], dma_sem=sem_st[tt]))
        assert sl_state["use"] == len(specs)
        for i in range(8):
            k = id(sem_st[i])
            P.eng["sp"].prog.append(("wait", sem_st[i], P.dma_count[k]))

        for n, fn in (("pe", block.tensor), ("act", block.scalar), ("dve", block.vector), ("pool", block.gpsimd), ("sp", block.sync)):
            fn(lambda e, n=n: P.run_engine(n, e))
    return nc


_NC_CACHE = {}


def _prep_inputs(inputs):
    f = lambda a: np.ascontiguousarray(np.asarray(a, dtype=np.float32))
    x = f(inputs["x"])
    shared = {
        "norm1_g": f(inputs["norm1_g"]).reshape(D),
        "w_in": f(inputs["w_in"]).reshape(D, DIN),
        "w_gate_up": f(inputs["w_gate_up"]).reshape(16, 512),
        "b_gate": f(inputs["b_gate"]).reshape(1, 512),
        "conv_w": f(inputs["conv_w"]).reshape(1024, 3),
        "conv_norm_g": f(inputs["conv_norm_g"]).reshape(1024),
        "gla_norm_g": f(inputs["gla_norm_g"]).reshape(1, 256),
        "w_out": f(inputs["w_out"]).reshape(D, D),
        "norm2_g": f(inputs["norm2_g"]).reshape(D),
        "w_ff1": f(inputs["w_ff1"]).reshape(D, DFF),
        "w_ff2": f(inputs["w_ff2"]).reshape(DFF, D),
        "norm_f_g": f(inputs["norm_f_g"]).reshape(1, D),
    }
    in_maps = []
    for c in range(NCORE):
        b, q = c // 4, c % 4
        m = dict(shared)
        m["x"] = np.ascontiguousarray(x[b, q * TOK:(q + 1) * TOK, :])
        xhalo = np.zeros((128, D), np.float32)
        if q > 0:
            xhalo[126:128] = x[b, q * TOK - 2:q * TOK, :]
        m["xh"] = xhalo
        cm = np.zeros((128, 8), np.float32)
        for r in range(NCORE):
            if r // 4 == b and r < c:
                cm[:, r] = 1.0
        m["cmask"] = cm
        in_maps.append(m)
    return in_maps


def kernel(**inputs):
    if "nc" not in _NC_CACHE:
        _NC_CACHE["nc"] = build_program()
    nc = _NC_CACHE["nc"]
    in_maps = _prep_inputs(inputs)
    res = run_bass_kernel_spmd(nc, in_maps, core_ids=list(range(NCORE)))
    outp = np.empty((2, 8192, D), np.float32)
    for c in range(NCORE):
        b, q = c // 4, c % 4
        outp[b, q * TOK:(q + 1) * TOK, :] = np.asarray(res.results[c]["out"], dtype=np.float32).reshape(TOK, D)
    return outp
```

```python
import numpy as np
from contextlib import ExitStack
import concourse.bass as bass
import concourse.mybir as mybir
from concourse.bass_utils import run_bass_kernel_spmd

F32 = mybir.dt.float32
BF16 = mybir.dt.bfloat16
AF = mybir.ActivationFunctionType
ALU = mybir.AluOpType

NCORE = 8
D = 2048
DIN = 6160
DFF = 8192
TOK = 2048
TT = 1024
KC = 16
EPS = 1e-6
NB = 2
_NO_CC = False
import os as _os
_DBG = _os.environ.get("KDBG", "")


class Op:
    __slots__ = ("sem", "val")

    def __init__(self, sem, val):
        self.sem = sem
        self.val = val


class Buf:
    __slots__ = ("name", "w", "r", "excl")

    def __init__(self, name, excl=False):
        self.name = name
        self.w = None
        self.r = {}
        self.excl = excl


class _Eng:
    def __init__(self, name, sem):
        self.name = name
        self.sem = sem
        self.count = 0
        self.prog = []
        self.waited = {}


class Prog:
    ENGS = ("pe", "act", "dve", "pool", "sp")

    def __init__(self, sems):
        self.eng = {n: _Eng(n, sems[n]) for n in self.ENGS}
        self.dma_count = {}
        self.bar = {}

    @staticmethod
    def _need(need, op):
        if op is None:
            return
        k = id(op.sem)
        if k not in need or need[k].val < op.val:
            need[k] = op

    def emit(self, eng, fn, reads=(), writes=(), deps=(), signal=True, dma_sem=None, ndma=1):
        E = self.eng[eng]
        need = {}
        xr = [b for b in reads if b.excl]
        if xr:
            reads = [b for b in reads if not b.excl]
            writes = list(writes) + xr
        for b in reads:
            self._need(need, b.w)
        for b in writes:
            self._need(need, b.w)
            for r in b.r.values():
                self._need(need, r)
        for d in deps:
            self._need(need, d)
        for d in self.bar.values():
            self._need(need, d)
        for k, op in need.items():
            if eng == "pe" and op.sem is E.sem:
                continue
            if E.waited.get(k, 0) < op.val:
                E.waited[k] = op.val
                E.prog.append(("wait", op.sem, op.val))
        if dma_sem is not None:
            k = id(dma_sem)
            self.dma_count[k] = self.dma_count.get(k, 0) + 16 * ndma
            op = Op(dma_sem, self.dma_count[k])
            E.prog.append(("dma", fn, dma_sem))
        elif signal:
            E.count += 1
            op = Op(E.sem, E.count)
            E.prog.append(("sig", fn, E.sem))
        else:
            op = None
            E.prog.append(("raw", fn, None))
        if op is not None:
            k = id(op.sem)
            for b in reads:
                if k not in b.r or b.r[k].val < op.val:
                    b.r[k] = op
            for b in writes:
                b.w = op
                b.r = {}
        return op

    def barrier(self, extra=()):
        self.bar = {}
        for E in self.eng.values():
            if E.name in ("sp", "pool"):
                continue
            if E.count > 0:
                self._need(self.bar, Op(E.sem, E.count))
        for op in extra:
            self._need(self.bar, op)

    def run_engine(self, name, e):
        for item in self.eng[name].prog:
            kind = item[0]
            if kind == "wait":
                e.wait_ge(item[1], item[2])
            elif kind == "dma":
                r = item[1](e)
                if isinstance(r, (list, tuple)):
                    for ins in r:
                        ins.then_inc(item[2], 16)
                else:
                    r.then_inc(item[2], 16)
            elif kind == "sig":
                item[1](e).then_inc(item[2], 1)
            else:
                item[1](e)


def build_program():
    nc = bass.Bass("TRN2", target_bir_lowering=False)

    def din(name, shape):
        return nc.dram_tensor(name, shape, F32, kind="ExternalInput").ap()

    x = din("x", [TOK, D])
    xh = din("xh", [128, D])
    norm1_g = din("norm1_g", [D])
    w_in = din("w_in", [D, DIN])
    w_gate_up = din("w_gate_up", [16, 512])
    b_gate = din("b_gate", [1, 512])
    conv_w = din("conv_w", [1024, 3])
    conv_norm_g = din("conv_norm_g", [1024])
    gla_norm_g = din("gla_norm_g", [1, 256])
    w_out = din("w_out", [D, D])
    norm2_g = din("norm2_g", [D])
    w_ff1 = din("w_ff1", [D, DFF])
    w_ff2 = din("w_ff2", [DFF, D])
    norm_f_g = din("norm_f_g", [1, D])
    cmask_d = din("cmask", [128, 8])
    out = nc.dram_tensor("out", [TOK, D], F32, kind="ExternalOutput").ap()
    xin = nc.dram_tensor("xch_in", [128, 1032], F32)
    xout = nc.dram_tensor("xch_out", [128 * NCORE, 1032], F32)

    with ExitStack() as es:
        def sb(name, shape, dt=F32):
            return es.enter_context(nc.sbuf_tensor(name, shape, dt))

        slab = [sb(f"slab{i}", [128, KC, 512], BF16) for i in range(NB)]
        uT = sb("uT", [128, KC, TT], BF16)
        yT = sb("yT", [128, KC, TT], BF16)
        arena = sb("arena", [128, 16384], F32)
        xt = [sb(f"xt{i}", [128, D], F32) for i in range(2)]
        xb = [sb(f"xb{i}", [128, D], BF16) for i in range(2)]
        Sx = sb("Sx", [128, 1032], F32)
        Sb_ = sb("Sb", [128, 2, 4, 256], BF16)
        gf_rep = sb("gf_rep", [128, D], F32)
        glag_rep = sb("glag_rep", [128, 256], F32)
        wg_ext = sb("wg_ext", [128, 512], F32)
        alT = sb("alT", [128, 512], F32)
        identb = sb("identb", [128, 128], BF16)
        ones128 = sb("ones128", [128, 128], F32)
        Mrev = sb("Mrev", [128, 128], F32)
        Ind = sb("Ind", [128, 2], F32)
        g1T = sb("g1T", [128, KC], F32)
        g2T = sb("g2T", [128, KC], F32)
        convw = sb("convw", [128, 8, 3], F32)
        convg = sb("convg", [128, 8], F32)
        cmask = sb("cmask_sb", [128, 8], F32)
        halo = sb("halo", [128, 8, 2], F32)
        stats = sb("stats", [128, 64], F32)
        decs = sb("decs", [128, 64], F32)
        w_al = sb("w_al", [128, KC, 16], BF16)

        ps = [es.enter_context(nc.psum_tensor(f"ps{i}", [128, 512], F32)) for i in range(8)]
        psb = [p[:, :].bitcast(BF16) for p in ps]

        sems = {n: es.enter_context(nc.semaphore(f"s_{n}")) for n in Prog.ENGS}
        sem_slab = [es.enter_context(nc.semaphore(f"s_slab{i}")) for i in range(NB)]
        sem_xt = [es.enter_context(nc.semaphore(f"s_xt{i}")) for i in range(2)]
        sem_xr = [es.enter_context(nc.semaphore(f"s_xr{i}")) for i in range(8)]
        sem_st = [es.enter_context(nc.semaphore(f"s_st{i}")) for i in range(8)]
        sem_gb = [es.enter_context(nc.semaphore(f"s_gb{i}")) for i in range(2)]
        sem_c = [es.enter_context(nc.semaphore(f"s_c{i}")) for i in range(4)]
        sem_cc = es.enter_context(nc.semaphore("s_cc"))
        block = es.enter_context(nc.Block())

        P = Prog(sems)

        B_slab = [Buf(f"slab{i}") for i in range(NB)]
        B_uT = [Buf(f"uT{i}") for i in range(8)]
        B_yTc = [Buf(f"yTc{i}") for i in range(KC)]
        B_ps = [Buf(f"ps{i}", excl=True) for i in range(8)]
        B_xt = [Buf(f"xt{i}") for i in range(2)]
        B_xb = [Buf(f"xb{i}") for i in range(2)]
        B_st = [Buf(f"st{i}") for i in range(2)]
        B_xres = [Buf(f"xres{i}") for i in range(8)]
        B_S = [Buf(f"S{h}") for h in range(4)]
        B_Sb = [[Buf(f"Sb{i}_{h}") for h in range(4)] for i in range(2)]
        B_Dtot = Buf("Dtot")
        B_const = Buf("const")
        B_alT = Buf("alT")
        B_halo = Buf("halo")
        B_xin = Buf("xin")
        B_gath = Buf("gath")

        def tmpbufs(prefix, n):
            return [Buf(f"{prefix}{i}") for i in range(n)]

        def aview(off_bytes, nbytes, dt=F32):
            a = arena[:, off_bytes // 4:(off_bytes + nbytes) // 4]
            return a if dt == F32 else a.bitcast(dt)

        xres = arena[:, :].rearrange("p (t d) -> p t d", t=8)
        kd = aview(0, 8192, BF16).rearrange("p (t c) -> p t c", t=8)
        vb = aview(8192, 16384, BF16).rearrange("p (t c) -> p t c", t=8)
        qT = aview(24576, 8192, BF16).rearrange("p (h t) -> p h t", h=4)
        gate = aview(32768, 32768).rearrange("p (t c) -> p t c", t=8)
        CV = 24576
        cv_cc = [aview(CV + i * 2048, 2048) for i in range(2)]
        cv_uc = [aview(CV + 4096 + i * 2064, 2064) for i in range(2)]
        cv_tmp = [aview(CV + 8224 + i * 2048, 2048) for i in range(2)]
        cv_yc = [aview(CV + 12320 + i * 2048, 2048) for i in range(2)]
        cv_sq = [aview(CV + 16416 + i * 2048, 2048) for i in range(2)]
        cv_rr = [aview(CV + 20512 + i * 2048, 2048) for i in range(2)]
        gbuf = [aview(49408 + i * 4128, 4128) for i in range(2)]
        uhT = aview(59648, 4096, BF16).rearrange("p (k t) -> p k t", k=KC)
        identf = aview(49152, 512)
        grow = aview(49664, 1536).rearrange("p (j c) -> p j c", j=3)
        kd1 = aview(24576, 8192, BF16).rearrange("p (t c) -> p t c", t=8)
        vb1 = aview(32768, 16384, BF16).rearrange("p (t c) -> p t c", t=8)
        dec1 = aview(59392, 256)
        B_cc, B_uc, B_tmp, B_yc, B_sq, B_rr = (tmpbufs(n, 2) for n in ("cc", "uc", "tmp", "yc", "sq", "rr"))
        B_uhT = Buf("uhT")
        B_gb = tmpbufs("gb", 2)
        tmpG = {"e1": [aview(49152, 2048), aview(49152, 2048)],
                "sp": [aview(51200 + i * 2048, 2048) for i in range(2)],
                "ex": [aview(55296 + i * 2048, 2048) for i in range(2)]}
        tmpM = {"e1": [xt[0][:, i * 512:(i + 1) * 512] for i in range(2)],
                "sp": [xt[0][:, 1024 + i * 512:1024 + (i + 1) * 512] for i in range(2)],
                "ex": [xt[1][:, i * 512:(i + 1) * 512] for i in range(2)],
                "og": [xt[1][:, 1024 + i * 512:1024 + (i + 1) * 512] for i in range(2)],
                "ytm": [xb[1][:, i * 1024:(i + 1) * 1024] for i in range(2)]}
        B_qT = [Buf(f"qT{h}") for h in range(4)]
        B_kd = [Buf(f"kd{t}") for t in range(8)]
        B_vb = [Buf(f"vb{t}") for t in range(8)]
        B_gate = [Buf(f"gate{t}") for t in range(8)]
        B_e1m, B_sp, B_ex, B_og, B_ytm = (tmpbufs(n, 2) for n in ("e1", "sp", "ex", "og", "ytm"))
        B_e1g = Buf("e1g")
        B_dec = [Buf(f"dec{t}") for t in range(8)]
        B_kd1 = [Buf(f"kd1_{t}") for t in range(8)]
        B_vb1 = [Buf(f"vb1_{t}") for t in range(8)]
        B_dec1 = [Buf(f"dec1_{t}") for t in range(8)]
        kvset0 = (kd, vb, decs, B_kd, B_vb, B_dec)
        kvset1 = (kd1, vb1, dec1, B_kd1, B_vb1, B_dec1)
        ftmp = [xb[0][:, i * 1024:(i + 1) * 1024].bitcast(F32) for i in range(2)]
        B_ftmp = tmpbufs("ftmp", 2)

        st_ss = [stats[:, 0:1], stats[:, 1:2]]
        st_rs = [stats[:, 2:3], stats[:, 3:4]]
        st_ss4 = [stats[:, 8:12], stats[:, 12:16]]
        st_rs4 = [stats[:, 16:20], stats[:, 20:24]]
        st_coef = stats[:, 24:28]
        st_hc = stats[:, 32:36]
        convg_s = stats[:, 40:48]
        B_st4 = tmpbufs("st4", 2)
        B_coef = Buf("coef")

        specs = []

        def rows_cols(w, r0, c0, ncols):
            return w[r0:r0 + 2048, c0:c0 + ncols].rearrange("(k p) c -> p k c", p=128)

        def spec_plain(tag, w, r0, c0, ncols=512):
            src = rows_cols(w, r0, c0, ncols)
            specs.append((tag, lambda e, dst, src=src, n=ncols: e.dma_start(out=dst[:, :, 0:n], in_=src), 1))

        def spec_conv(g):
            srcs = [rows_cols(w_in, 0, j * 1024 + g * 128, 128) for j in range(3)]
            specs.append((("conv", g), lambda e, dst, srcs=srcs: [e.dma_start(out=dst[:, :, j * 128:(j + 1) * 128], in_=srcs[j]) for j in range(3)], 3))

        def plan_gla(readout, skip_kv=False, v_first=False):
            if readout:
                spec_plain("q", w_in, 0, 3072)
            if not skip_kv:
                if not v_first:
                    spec_plain("k", w_in, 0, 3584)
                spec_plain("v0", w_in, 0, 4096)
                spec_plain("v1", w_in, 0, 4608)
                if v_first:
                    spec_plain("k", w_in, 0, 3584)
            if readout:
                spec_plain("og0", w_in, 0, 5120)
                spec_plain("og1", w_in, 0, 5632)

        for sh in range(2):
            plan_gla(False, v_first=(sh == 0))
        for t in range(2):
            for g in range(8):
                spec_conv(g)
            plan_gla(True, skip_kv=(t == 0))
            for cb in range(4):
                spec_plain(("wout", cb), w_out, 0, cb * 512)
            for fb in range(4):
                for fs in range(4):
                    spec_plain(("w1", fb, fs), w_ff1, 0, fb * 2048 + fs * 512)
                for cb in range(4):
                    spec_plain(("w2", fb, cb), w_ff2, fb * 2048, cb * 512)

        sl_state = {"issue": 0, "use": 0}

        def slab_issue(j):
            slot = j % NB
            tag, fn, nd = specs[j]
            P.emit("pool", lambda e, fn=fn, slot=slot: fn(e, slab[slot]), writes=[B_slab[slot]],
                   dma_sem=sem_slab[slot], ndma=nd)

        def slab_prefetch():
            i = sl_state["use"]
            while sl_state["issue"] < min(i + NB, len(specs)):
                slab_issue(sl_state["issue"])
                sl_state["issue"] += 1

        def slab_use(tag):
            i = sl_state["use"]
            assert specs[i][0] == tag, (specs[i][0], tag)
            slab_prefetch()
            sl_state["use"] += 1
            return i % NB

        rot = {"big": 0, "pi": 0}

        def mm_group(bank, out_ap, pairs, reads):
            n = len(pairs)
            op = None
            for i, (l, r) in enumerate(pairs):
                op = P.emit("pe", lambda e, l=l, r=r, i=i: e.matmul(out_ap, lhsT=l, rhs=r, start=(i == 0), stop=(i == n - 1)),
                            reads=reads, writes=[B_ps[bank]], signal=(i == n - 1))
            return op

        def norm_head(src_ap, B_src, i):
            al = B_ftmp if i == 0 else []
            P.emit("act", lambda e: e.activation(out=xb[i][:], in_=src_ap, func=AF.Square, accum_out=st_ss[i]),
                   reads=[B_src], writes=[B_xb[i], B_st[i]] + al)
            P.emit("act", lambda e: e.activation(out=st_rs[i], in_=st_ss[i], func=AF.Ln, scale=1.0 / D, bias=EPS),
                   reads=[B_st[i]], writes=[B_st[i]])
            P.emit("act", lambda e: e.activation(out=st_rs[i], in_=st_rs[i], func=AF.Exp, scale=-0.5),
                   reads=[B_st[i]], writes=[B_st[i]])
            P.emit("act", lambda e: e.activation(out=xb[i][:], in_=src_ap, func=AF.Copy, scale=st_rs[i]),
                   reads=[B_src, B_st[i]], writes=[B_xb[i]])

        def norm_tail(gTt, dstT, dst_tok0, dst_tiles, i, hi_banks=False):
            banks = (0, 1) if i == 0 else (2, 3)
            if hi_banks:
                banks = (4, 5) if i == 0 else (6, 7)
            for half in range(2):
                bk = banks[half]
                for j in range(8):
                    k = half * 8 + j
                    P.emit("pe", lambda e, k=k, j=j, bk=bk: e.transpose(out=psb[bk][:, j * 128:(j + 1) * 128],
                                                                         in_=xb[i][:, k * 128:(k + 1) * 128], identity=identb[:]),
                           reads=[B_xb[i], B_const], writes=[B_ps[bk]], signal=(j == 7))
                P.emit("dve", lambda e, half=half, bk=bk: e.tensor_tensor(
                    out=dstT[:, half * 8:(half + 1) * 8, dst_tok0:dst_tok0 + 128],
                    in0=psb[bk][:, 0:1024].rearrange("p (k t) -> p k t", k=8),
                    in1=gTt[:, half * 8:(half + 1) * 8].unsqueeze(2).to_broadcast([128, 8, 128]), op=ALU.mult),
                    reads=[B_ps[bk], B_const], writes=dst_tiles)

        def x_tile_tasks(src_dram, row0, gTt, dstT, tok0, dst_tiles, hi_banks=False):
            st = {}

            def head():
                i = rot["pi"] % 2
                rot["pi"] += 1
                st["i"] = i
                P.emit("sp", lambda e: e.dma_start(out=xt[i][:], in_=src_dram[row0:row0 + 128, :]), writes=[B_xt[i]], dma_sem=sem_xt[i])
                norm_head(xt[i][:], B_xt[i], i)

            def tail():
                norm_tail(gTt, dstT, tok0, dst_tiles, st["i"], hi_banks=hi_banks)
            return head, tail

        def lagged(tasks):
            hs, ts = tasks[0::2], tasks[1::2]
            out_ = [hs[0]]
            for j in range(1, len(hs)):
                out_.append(hs[j])
                out_.append(ts[j - 1])
            out_.append(ts[-1])
            return out_

        def run_tasks(tasks):
            for f in tasks:
                f()

        class Interleaver:
            def __init__(self, tasks, nslots):
                self.tasks = list(tasks)
                self.n = nslots
                self.done = 0
                self.slot = 0

            def tick(self):
                self.slot += 1
                want = (len(self.tasks) * self.slot) // self.n if self.n > 0 else len(self.tasks)
                while self.done < min(want, len(self.tasks)):
                    self.tasks[self.done]()
                    self.done += 1

            def flush(self):
                while self.done < len(self.tasks):
                    self.tasks[self.done]()
                    self.done += 1

        first_tasks = []
        for tt in range(8):
            first_tasks.extend(x_tile_tasks(x, tt * 128, g1T, uT, tt * 128, [B_uT[tt]]))
        B_grow = Buf("grow")
        B_cpool = Buf("cpool")
        B_wal = Buf("wal")
        slab_prefetch()
        P.emit("pool", lambda e: e.memset(wg_ext[:], 0.0), writes=[B_const])
        P.emit("act", lambda e: e.dma_start(out=grow[0:16, 0, :], in_=norm1_g.rearrange("(k p) -> k p", p=128)), writes=[B_grow], dma_sem=sem_c[0])
        P.emit("act", lambda e: e.dma_start(out=grow[0:16, 1, :], in_=norm2_g.rearrange("(k p) -> k p", p=128)), writes=[B_grow], dma_sem=sem_c[0])
        P.emit("act", lambda e: e.dma_start(out=grow[0:8, 2, :], in_=conv_norm_g.rearrange("(k p) -> k p", p=128)), writes=[B_grow], dma_sem=sem_c[0])
        P.emit("act", lambda e: e.dma_start(out=wg_ext[0:16, :], in_=w_gate_up[:, :]), writes=[B_const], dma_sem=sem_c[3])
        P.emit("act", lambda e: e.dma_start(out=wg_ext[16:17, :], in_=b_gate[:, :]), writes=[B_const], dma_sem=sem_c[3])
        P.emit("act", lambda e: e.dma_start(out=convw[:], in_=conv_w.rearrange("(g p) j -> p g j", p=128)), writes=[B_const], dma_sem=sem_c[3])
        P.emit("act", lambda e: e.dma_start(out=cmask[:], in_=cmask_d[:, :]), writes=[B_const], dma_sem=sem_c[3])
        P.emit("act", lambda e: e.dma_start(out=glag_rep[:], in_=gla_norm_g.broadcast_to([128, 256])), writes=[B_const], dma_sem=sem_c[3])
        P.emit("act", lambda e: e.dma_start(out=gf_rep[:], in_=norm_f_g.broadcast_to([128, D])), writes=[B_const], dma_sem=sem_c[3])
        first_tasks = lagged(first_tasks)
        first_tasks[0]()
        P.emit("pool", lambda e: e.memset(identf, 1.0), writes=[B_cpool])
        P.emit("pool", lambda e: e.affine_select(out=identf, in_=identf, pattern=[[-1, 128]], compare_op=ALU.is_equal,
                                                 fill=0.0, base=0, channel_multiplier=1), reads=[B_cpool], writes=[B_cpool])
        P.emit("pool", lambda e: e.memset(ones128[:], 1.0), reads=[B_cpool], writes=[B_cpool])
        P.emit("pool", lambda e: e.memset(Mrev[:], -1.0 / 16), reads=[B_cpool], writes=[B_cpool])
        P.emit("pool", lambda e: e.affine_select(out=Mrev[:], in_=Mrev[:], pattern=[[-1, 128]], compare_op=ALU.is_gt,
                                                 fill=0.0, base=0, channel_multiplier=1), reads=[B_cpool], writes=[B_cpool])
        P.emit("pool", lambda e: e.memset(Mrev[64:128, 0:64], 0.0), reads=[B_cpool], writes=[B_cpool])
        P.emit("pool", lambda e: e.memset(Ind[:], 0.0), reads=[B_cpool], writes=[B_cpool])
        P.emit("pool", lambda e: e.memset(Ind[0:64, 0:1], -1.0 / 16), reads=[B_cpool], writes=[B_cpool])
        P.emit("pool", lambda e: e.memset(Ind[64:128, 1:2], -1.0 / 16), reads=[B_cpool], writes=[B_cpool])
        P.emit("pool", lambda e: e.memset(alT[:], 1.0), writes=[B_alT])
        P.emit("pool", lambda e: e.affine_select(out=alT[:], in_=alT[:], pattern=[[0, 512]], compare_op=ALU.is_equal,
                                                 fill=0.0, base=-16, channel_multiplier=1), reads=[B_alT], writes=[B_alT])
        P.emit("pool", lambda e: e.memset(Sx[:, 0:1024], 0.0), writes=B_S)
        P.emit("pool", lambda e: e.memset(Sx[:, 1024:1032], 1.0), writes=[B_Dtot])
        P.emit("dve", lambda e: e.tensor_copy(out=identb[:], in_=identf), reads=[B_cpool], writes=[B_const])
        P.emit("pool", lambda e: e.dma_start(out=w_al[:], in_=rows_cols(w_in, 0, 6144, 16)), writes=[B_wal], dma_sem=sem_c[2])
        for j, (dst, n) in enumerate(((g1T, 16), (g2T, 16), (convg, 8))):
            P.emit("pe", lambda e, j=j, n=n: e.transpose(out=ps[7][:, j * 16:j * 16 + n], in_=grow[0:n, j, :], identity=identf[0:n, 0:n]),
                   reads=[B_grow, B_cpool], writes=[B_ps[7]])
            P.emit("dve", lambda e, j=j, n=n, dst=dst: e.tensor_copy(out=dst[:, 0:n], in_=ps[7][:, j * 16:j * 16 + n]), reads=[B_ps[7]], writes=[B_const])
        P.emit("dve", lambda e: e.tensor_scalar(out=convg_s, in0=convg[:], scalar1=float(np.sqrt(128.0)), scalar2=None, op0=ALU.mult),
               reads=[B_const], writes=[B_const])

        def gla(readout, tm, ilv, kvset=None, skip_kv=False, pre_tasks=None):
            uts = B_uT
            kd, vb, decs, B_kd, B_vb, B_dec = kvset if kvset is not None else kvset0
            B_e1 = B_e1m if tm is tmpM else [B_e1g, B_e1g]
            if readout:
                s = slab_use("q")
                for h in range(4):
                    for nt in range(2):
                        bk = rot["big"] % 4
                        rot["big"] += 1
                        mm_group(bk, ps[bk][:, :], [(slab[s][:, k, h * 128:(h + 1) * 128], uT[:, k, nt * 512:(nt + 1) * 512]) for k in range(KC)],
                                 reads=[B_slab[s]] + uts[nt * 4:(nt + 1) * 4])
                        P.emit("act", lambda e, h=h, bk=bk, nt=nt: e.activation(out=qT[:, h, nt * 512:(nt + 1) * 512], in_=ps[bk][:, :], func=AF.Copy,
                                                                               scale=float(128.0 ** -0.5)),
                               reads=[B_ps[bk]], writes=[B_qT[h]])
            pt = {"n": 0}

            def v_stage():
                for vs in range(2):
                    s = slab_use(f"v{vs}")
                    for tt in range(8):
                        if pre_tasks is not None and vs == 0:
                            upto = min(2 * tt + 2, len(pre_tasks) - 1) if tt < 7 else len(pre_tasks) - 1
                            while pt["n"] <= upto:
                                pre_tasks[pt["n"]]()
                                pt["n"] += 1
                        bk = rot["big"] % 4
                        rot["big"] += 1
                        t0 = tt * 128
                        mm_group(bk, ps[bk][:, :], [(uT[:, k, t0:t0 + 128], slab[s][:, k, :]) for k in range(KC)], reads=[B_slab[s], uts[tt]])
                        P.emit("act", lambda e, tt=tt, vs=vs, bk=bk: e.activation(out=vb[:, tt, vs * 512:(vs + 1) * 512], in_=ps[bk][:, :], func=AF.Copy),
                               reads=[B_ps[bk]], writes=[B_vb[tt]])

            if pre_tasks is not None:
                v_stage()
            s_k = None if skip_kv else slab_use("k")

            def stage_a(tt):
                i2 = tt % 2
                if tt % 4 == 0:
                    nt = tt // 4
                    bk = rot["big"] % 4
                    rot["big"] += 1
                    mm_group(bk, ps[bk][0:16, :], [(w_al[:, k, :], uT[:, k, nt * 512:(nt + 1) * 512]) for k in range(KC)],
                             reads=[B_wal] + uts[nt * 4:(nt + 1) * 4])
                    P.emit("act", lambda e, bk=bk: e.activation(out=alT[0:16, :], in_=ps[bk][0:16, :], func=AF.Copy), reads=[B_ps[bk]], writes=[B_alT])
                pb = 4 + i2
                c0 = (tt % 4) * 128
                mm_group(pb, ps[pb][:, :], [(alT[:, c0:c0 + 128], wg_ext[:, :])], reads=[B_alT, B_const])
                P.emit("act", lambda e, i2=i2, pb=pb: e.activation(out=tm["e1"][i2], in_=ps[pb][:, :], func=AF.Exp, scale=-1.0), reads=[B_ps[pb]], writes=[B_e1[i2]])
                P.emit("act", lambda e, i2=i2: e.activation(out=tm["sp"][i2], in_=tm["e1"][i2], func=AF.Ln, bias=1.0), reads=[B_e1[i2]], writes=[B_sp[i2]])

            if not skip_kv:
                stage_a(0)
            for tt in (range(8) if not skip_kv else ()):
                i2 = tt % 2
                bk = rot["big"] % 4
                rot["big"] += 1
                t0 = tt * 128
                mm_group(bk, ps[bk][:, :], [(uT[:, k, t0:t0 + 128], slab[s_k][:, k, :]) for k in range(KC)], reads=[B_slab[s_k], uts[tt]])
                if tt + 1 < 8:
                    stage_a(tt + 1)
                mm_group(6, ps[6][:, :], [(Mrev[:, :], tm["sp"][i2])], reads=[B_const, B_sp[i2]])
                P.emit("act", lambda e, i2=i2: e.activation(out=tm["ex"][i2], in_=ps[6][:, :], func=AF.Exp), reads=[B_ps[6]], writes=[B_ex[i2]])
                P.emit("dve", lambda e, tt=tt, bk=bk, i2=i2: e.tensor_tensor(out=kd[:, tt, :], in0=ps[bk][:, :], in1=tm["ex"][i2], op=ALU.mult),
                       reads=[B_ps[bk], B_ex[i2]], writes=[B_kd[tt]])
                for h in range(4):
                    P.emit("pe", lambda e, h=h, i2=i2: e.matmul(ps[7][:, h * 2:h * 2 + 2], lhsT=tm["sp"][i2][:, h * 128:(h + 1) * 128], rhs=Ind[:, :], start=True, stop=True),
                           reads=[B_sp[i2], B_const], writes=[B_ps[7]], signal=(h == 3))
                P.emit("act", lambda e, tt=tt: e.activation(out=decs[:, tt * 8:(tt + 1) * 8], in_=ps[7][:, 0:8], func=AF.Exp), reads=[B_ps[7]], writes=[B_dec[tt]])
            if pre_tasks is None and not skip_kv:
                v_stage()
            if readout:
                for os_ in range(2):
                    s = slab_use(f"og{os_}")
                    for tt in range(8):
                        i2 = tt % 2
                        bk = rot["big"] % 4
                        rot["big"] += 1
                        t0 = tt * 128
                        mm_group(bk, ps[bk][:, :], [(uT[:, k, t0:t0 + 128], slab[s][:, k, :]) for k in range(KC)], reads=[B_slab[s], uts[tt]])
                        P.emit("act", lambda e, bk=bk, i2=i2: e.activation(out=tm["og"][i2], in_=ps[bk][:, :], func=AF.Silu), reads=[B_ps[bk]], writes=[B_og[i2]])
                        P.emit("dve", lambda e, tt=tt, os_=os_, i2=i2: e.tensor_tensor(
                            out=gate[:, tt, os_ * 512:(os_ + 1) * 512].rearrange("p (h d) -> p h d", h=2),
                            in0=tm["og"][i2].rearrange("p (h d) -> p h d", h=2),
                            in1=glag_rep[:].unsqueeze(1).to_broadcast([128, 2, 256]), op=ALU.mult),
                            reads=[B_og[i2], B_const], writes=[B_gate[tt]])
            slab_prefetch()

            def kv_update(ci):
                tt, c = ci // 2, ci % 2
                for h in range(4):
                    bk = h // 2
                    col = (h % 2) * 256
                    mm_group(bk, ps[bk][:, col:col + 256],
                             [(kd[64 * c:64 * c + 64, tt, h * 128:(h + 1) * 128], vb[64 * c:64 * c + 64, tt, h * 256:(h + 1) * 256])],
                             reads=[B_kd[tt], B_vb[tt]])
                for h in range(4):
                    bk = h // 2
                    col = (h % 2) * 256
                    di = tt * 8 + h * 2 + c
                    P.emit("dve", lambda e, h=h, di=di, bk=bk, col=col: e.scalar_tensor_tensor(
                        out=Sx[:, h * 256:(h + 1) * 256], in0=Sx[:, h * 256:(h + 1) * 256],
                        scalar=decs[:, di:di + 1], in1=ps[bk][:, col:col + 256], op0=ALU.mult, op1=ALU.add),
                        reads=[B_ps[bk], B_dec[tt], B_S[h]], writes=[B_S[h]])
                    if readout:
                        sbi = ci % 2
                        P.emit("act", lambda e, h=h, sbi=sbi: e.activation(out=Sb_[:, sbi, h, :], in_=Sx[:, h * 256:(h + 1) * 256], func=AF.Copy),
                               reads=[B_S[h]], writes=[B_Sb[sbi][h]])
                if not readout:
                    P.emit("dve", lambda e, tt=tt, c=c: e.tensor_tensor(
                        out=Sx[:, 1024:1028], in0=Sx[:, 1024:1028],
                        in1=decs[:, tt * 8:(tt + 1) * 8].rearrange("p (h c) -> p h c", c=2)[:, :, c], op=ALU.mult),
                        reads=[B_dec[tt], B_Dtot], writes=[B_Dtot])

            def obank(tt, h):
                return 4 + 2 * (tt % 2) + h // 2

            def read_out(ci):
                tt, c = ci // 2, ci % 2
                sbi = ci % 2
                for h in range(4):
                    ob = obank(tt, h)
                    t0 = tt * 128 + 64 * c
                    P.emit("pe", lambda e, h=h, ob=ob, t0=t0, c=c, sbi=sbi: e.matmul(ps[ob][64 * c:64 * c + 64, (h % 2) * 256:(h % 2) * 256 + 256],
                                                                                  lhsT=qT[:, h, t0:t0 + 64], rhs=Sb_[:, sbi, h, :], start=True, stop=True),
                           reads=[B_qT[h], B_Sb[sbi][h]], writes=[B_ps[ob]], signal=True)

            def tile_act(tt):
                i2 = tt % 2
                for h in range(4):
                    ob = obank(tt, h)
                    P.emit("act", lambda e, h=h, ob=ob, i2=i2: e.activation(out=tm["e1"][i2][:, 0:256], in_=ps[ob][:, (h % 2) * 256:(h % 2) * 256 + 256],
                                                                            func=AF.Square, accum_out=st_ss4[i2][:, h:h + 1]),
                           reads=[B_ps[ob]], writes=[B_e1[i2], B_st4[i2]])
                P.emit("act", lambda e, i2=i2: e.activation(out=st_rs4[i2], in_=st_ss4[i2], func=AF.Ln, scale=1.0 / 256, bias=EPS),
                       reads=[B_st4[i2]], writes=[B_st4[i2]])
                P.emit("act", lambda e, i2=i2: e.activation(out=st_rs4[i2], in_=st_rs4[i2], func=AF.Exp, scale=-0.5),
                       reads=[B_st4[i2]], writes=[B_st4[i2]])

            def tile_dve(tt):
                i2 = tt % 2
                for h in range(4):
                    ob = obank(tt, h)
                    P.emit("dve", lambda e, h=h, ob=ob, i2=i2, tt=tt: e.scalar_tensor_tensor(
                        out=tm["ytm"][i2][:, h * 256:(h + 1) * 256], in0=ps[ob][:, (h % 2) * 256:(h % 2) * 256 + 256],
                        scalar=st_rs4[i2][:, h:h + 1], in1=gate[:, tt, h * 256:(h + 1) * 256], op0=ALU.mult, op1=ALU.mult),
                        reads=[B_ps[ob], B_st4[i2], B_gate[tt]], writes=[B_ytm[i2]])

            def tile_tail(tt):
                i2 = tt % 2
                tb = 2 + (tt % 2)
                for j in range(8):
                    P.emit("pe", lambda e, j=j, tb=tb, i2=i2: e.transpose(out=psb[tb][:, j * 128:(j + 1) * 128],
                                                                         in_=tm["ytm"][i2][:, j * 128:(j + 1) * 128], identity=identb[:]),
                           reads=[B_ytm[i2], B_const], writes=[B_ps[tb]], signal=(j == 7))
                y0 = tt * 128
                P.emit("act", lambda e, tb=tb, y0=y0: e.activation(out=yT[:, 8:16, y0:y0 + 128],
                                                                   in_=psb[tb][:, 0:1024].rearrange("p (k t) -> p k t", k=8), func=AF.Copy),
                       reads=[B_ps[tb]], writes=B_yTc[8:16])

            kv_update(0)
            pending = []
            for ci in range(16):
                if ci + 1 < 16:
                    kv_update(ci + 1)
                if readout:
                    read_out(ci)
                    if ci % 2 == 1:
                        tile_act(ci // 2)
                        pending.append((ci + 1, tile_dve, ci // 2))
                        pending.append((ci + 2, tile_tail, ci // 2))
                    due = [p for p in pending if p[0] <= ci]
                    pending = [p for p in pending if p[0] > ci]
                    for _, fn, a in due:
                        fn(a)
                if ilv is not None:
                    ilv.tick()
            for _, fn, a in pending:
                fn(a)
            if ilv is not None:
                ilv.flush()

        for sh in range(2):
            if sh == 0:
                nxt = []
                for tt in range(8):
                    nxt.extend(x_tile_tasks(x, TT + tt * 128, g1T, uT, tt * 128, [B_uT[tt]], hi_banks=True))
            else:
                nxt = list(x_tile_tasks(xh, 0, g1T, uhT, 0, [B_uhT], hi_banks=True))
                for tt in range(8):
                    nxt.extend(x_tile_tasks(x, tt * 128, g1T, uT, tt * 128, [B_uT[tt]], hi_banks=True))
            gla(False, tmpG, Interleaver(lagged(nxt), 16), kvset=(kvset0 if sh == 0 else kvset1),
                pre_tasks=(first_tasks[1:] if sh == 0 else None))
        P.barrier()
        P.emit("sp", lambda e: e.dma_start(out=xin[:, :], in_=Sx[:, :]), reads=B_S + [B_Dtot], writes=[B_xin], dma_sem=sem_c[1])
        E = P.eng["pool"]
        opx = B_xin.w
        E.prog.append(("wait", opx.sem, opx.val))
        if _NO_CC:
            for r in range(8):
                E.prog.append(("raw", lambda e, r=r: e.dma_start(out=xout.ap()[r * 128:(r + 1) * 128, :], in_=xin.ap()[:, :]).then_inc(sem_cc, 16), None))
        else:
            E.prog.append(("raw", lambda e: e.collective_compute("AllGather", ALU.bypass, replica_groups=[list(range(NCORE))],
                                                                 ins=[xin.ap().opt()], outs=[xout.ap().opt()]).then_inc(sem_cc, 1), None))
        ccop = Op(sem_cc, 128 if _NO_CC else 1)

        def fold_state(ranks):
            if ranks[0] == 0:
                P.emit("dve", lambda e: e.memset(Sx[:, 0:1024], 0.0), reads=[B_xin], writes=B_S)
            for r in ranks:
                gi = r % 2
                P.emit("sp", lambda e, r=r, gi=gi: e.dma_start(out=gbuf[gi], in_=xout.ap()[r * 128:(r + 1) * 128, :]), deps=[ccop], writes=[B_gb[gi]], dma_sem=sem_gb[gi])
                P.emit("dve", lambda e, r=r, gi=gi: e.tensor_scalar(out=st_coef, in0=gbuf[gi][:, 1024:1028], scalar1=-1.0, scalar2=cmask[:, r:r + 1],
                                                                    op0=ALU.add, op1=ALU.mult), reads=[B_gb[gi], B_const], writes=[B_coef])
                P.emit("dve", lambda e: e.tensor_scalar(out=st_coef, in0=st_coef, scalar1=1.0, scalar2=None, op0=ALU.add), reads=[B_coef], writes=[B_coef])
                P.emit("dve", lambda e, r=r, gi=gi: e.tensor_scalar(out=gbuf[gi][:, 0:1024], in0=gbuf[gi][:, 0:1024], scalar1=cmask[:, r:r + 1], scalar2=None, op0=ALU.mult),
                       reads=[B_gb[gi], B_const], writes=[B_gb[gi]])
                for h in range(4):
                    P.emit("dve", lambda e, gi=gi, h=h: e.scalar_tensor_tensor(out=Sx[:, h * 256:(h + 1) * 256], in0=Sx[:, h * 256:(h + 1) * 256],
                                                                             scalar=st_coef[:, h:h + 1], in1=gbuf[gi][:, h * 256:(h + 1) * 256],
                                                                             op0=ALU.mult, op1=ALU.add),
                           reads=[B_gb[gi], B_coef, B_S[h]], writes=[B_S[h]])

        store_ops = []
        for t in range(2):
            if t == 0:
                P.barrier()
            pend = None
            for g in range(8):
                if t == 0 and g >= 2:
                    fold_state({2: [0], 3: [1], 4: [2, 3], 5: [4, 5], 6: [6], 7: [7]}[g])
                s = slab_use(("conv", g))
                if t == 0:
                    for j in (1, 2):
                        for k in range(KC):
                            P.emit("pe", lambda e, j=j, k=k, s=s: e.matmul(ps[6][:, (j - 1) * 2:(j - 1) * 2 + 2], lhsT=slab[s][:, k, j * 128:(j + 1) * 128],
                                                                            rhs=uhT[:, k, 126:128], start=(k == 0), stop=(k == KC - 1)),
                                   reads=[B_slab[s], B_uhT], writes=[B_ps[6]], signal=(k == KC - 1 and j == 2))
                    P.emit("act", lambda e: e.activation(out=st_hc[:, 0:2], in_=ps[6][:, 0:2], func=AF.Copy), reads=[B_ps[6]], writes=[B_coef])
                    P.emit("dve", lambda e, g=g: e.tensor_tensor(out=halo[:, g, :], in0=ps[6][:, 2:4], in1=st_hc[:, 0:2], op=ALU.mult),
                           reads=[B_ps[6], B_coef], writes=[B_halo])
                for nt in range(2):
                    i2 = nt
                    bks = (0, 1, 2) if nt == 0 else (3, 4, 5)
                    uts = [B_uT[nt * 4 + i] for i in range(4)]
                    for j in range(3):
                        mm_group(bks[j], ps[bks[j]][:, :], [(slab[s][:, k, j * 128:(j + 1) * 128], uT[:, k, nt * 512:(nt + 1) * 512]) for k in range(KC)],
                                 reads=[B_slab[s]] + uts)
                    if pend is not None:
                        pend()
                    cbk, cck, chk = bks
                    P.emit("act", lambda e, i2=i2, cck=cck: e.activation(out=cv_cc[i2], in_=ps[cck][:, :], func=AF.Copy), reads=[B_ps[cck]], writes=[B_cc[i2]])
                    P.emit("act", lambda e, i2=i2, g=g: e.activation(out=cv_uc[i2][:, 0:2], in_=halo[:, g, :], func=AF.Copy), reads=[B_halo], writes=[B_uc[i2]])
                    P.emit("dve", lambda e, i2=i2, chk=chk: e.tensor_tensor(out=cv_uc[i2][:, 2:514], in0=ps[chk][:, :], in1=cv_cc[i2], op=ALU.mult),
                           reads=[B_ps[chk], B_cc[i2], B_uc[i2]], writes=[B_uc[i2]])
                    P.emit("act", lambda e, i2=i2, g=g: e.activation(out=halo[:, g, :], in_=cv_uc[i2][:, 512:514], func=AF.Copy), reads=[B_uc[i2]], writes=[B_halo])
                    P.emit("dve", lambda e, i2=i2, g=g: e.tensor_scalar(out=cv_tmp[i2], in0=cv_uc[i2][:, 2:514], scalar1=convw[:, g, 2:3], scalar2=None, op0=ALU.mult),
                           reads=[B_uc[i2], B_const], writes=[B_tmp[i2]])
                    P.emit("dve", lambda e, i2=i2, g=g: e.scalar_tensor_tensor(out=cv_tmp[i2], in0=cv_uc[i2][:, 1:513], scalar=convw[:, g, 1:2], in1=cv_tmp[i2],
                                                                              op0=ALU.mult, op1=ALU.add),
                           reads=[B_uc[i2], B_const, B_tmp[i2]], writes=[B_tmp[i2]])
                    P.emit("dve", lambda e, i2=i2, g=g: e.scalar_tensor_tensor(out=cv_tmp[i2], in0=cv_uc[i2][:, 0:512], scalar=convw[:, g, 0:1], in1=cv_tmp[i2],
                                                                              op0=ALU.mult, op1=ALU.add),
                           reads=[B_uc[i2], B_const, B_tmp[i2]], writes=[B_tmp[i2]])
                    P.emit("dve", lambda e, i2=i2, cbk=cbk: e.tensor_tensor(out=cv_yc[i2], in0=ps[cbk][:, :], in1=cv_tmp[i2], op=ALU.mult),
                           reads=[B_ps[cbk], B_tmp[i2]], writes=[B_yc[i2]])
                    P.emit("act", lambda e, i2=i2: e.activation(out=cv_sq[i2], in_=cv_yc[i2], func=AF.Square), reads=[B_yc[i2]], writes=[B_sq[i2]])

                    def conv_tail(i2=i2, g=g, nt=nt):
                        sbk = 6 + i2
                        mm_group(sbk, ps[sbk][:, :], [(ones128[:, :], cv_sq[i2])], reads=[B_const, B_sq[i2]])
                        P.emit("act", lambda e: e.activation(out=cv_rr[i2], in_=ps[sbk][:, :], func=AF.Ln, scale=1.0, bias=128.0 * EPS),
                               reads=[B_ps[sbk]], writes=[B_rr[i2]])
                        P.emit("act", lambda e: e.activation(out=cv_rr[i2], in_=cv_rr[i2], func=AF.Exp, scale=-0.5), reads=[B_rr[i2]], writes=[B_rr[i2]])
                        P.emit("dve", lambda e: e.scalar_tensor_tensor(out=yT[:, g, nt * 512:(nt + 1) * 512], in0=cv_yc[i2], scalar=convg_s[:, g:g + 1],
                                                                       in1=cv_rr[i2], op0=ALU.mult, op1=ALU.mult),
                               reads=[B_yc[i2], B_rr[i2], B_const], writes=[B_yTc[g]])
                    if "c" in _DBG:
                        conv_tail()
                        pend = None
                    else:
                        pend = conv_tail
            if pend is not None:
                pend()
            P.barrier(extra=store_ops)
            gla(True, tmpM, None, skip_kv=(t == 0))
            P.barrier()
            for tt in range(8):
                r0 = t * TT + tt * 128
                P.emit("sp", lambda e, tt=tt, r0=r0: e.dma_start(out=xres[:, tt, :], in_=x[r0:r0 + 128, :]), writes=[B_xres[tt]], dma_sem=sem_xr[tt])
            for cb in range(4):
                s = slab_use(("wout", cb))
                for tt in range(8):
                    bk = rot["big"] % 8
                    rot["big"] += 1
                    mm_group(bk, ps[bk][:, :], [(yT[:, k, tt * 128:(tt + 1) * 128], slab[s][:, k, :]) for k in range(KC)], reads=[B_slab[s]] + B_yTc)
                    P.emit("dve", lambda e, tt=tt, cb=cb, bk=bk: e.tensor_tensor(out=xres[:, tt, cb * 512:(cb + 1) * 512], in0=xres[:, tt, cb * 512:(cb + 1) * 512],
                                                                               in1=ps[bk][:, :], op=ALU.add),
                           reads=[B_ps[bk], B_xres[tt]], writes=[B_xres[tt]])
            prev = None
            for tt in range(8):
                i = rot["pi"] % 2
                rot["pi"] += 1
                norm_head(xres[:, tt, :], B_xres[tt], i)
                if prev is not None:
                    norm_tail(g2T, uT, prev[0] * 128, [B_uT[prev[0]]], prev[1])
                prev = (tt, i)
            norm_tail(g2T, uT, prev[0] * 128, [B_uT[prev[0]]], prev[1])
            ilv = None
            for fb in range(4):
                for fs in range(4):
                    s = slab_use(("w1", fb, fs))
                    for fc in range(4):
                        j = fs * 4 + fc
                        for nt in range(2):
                            bk = rot["big"] % 8
                            rot["big"] += 1
                            i2 = rot["big"] % 2
                            mm_group(bk, ps[bk][:, :], [(slab[s][:, k, fc * 128:(fc + 1) * 128], uT[:, k, nt * 512:(nt + 1) * 512]) for k in range(KC)],
                                     reads=[B_slab[s]] + B_uT[nt * 4:(nt + 1) * 4])
                            P.emit("act", lambda e, bk=bk, i2=i2: e.activation(out=ftmp[i2], in_=ps[bk][:, :], func=AF.Relu), reads=[B_ps[bk]],
                                   writes=[B_ftmp[i2], B_xb[0]])
                            P.emit("dve", lambda e, j=j, nt=nt, i2=i2: e.tensor_tensor(out=yT[:, j, nt * 512:(nt + 1) * 512], in0=ftmp[i2], in1=ftmp[i2], op=ALU.mult),
                                   reads=[B_ftmp[i2]], writes=[B_yTc[j]])
                if fb == 3 and t == 0:
                    nxt = []
                    for tt in range(8):
                        nxt.extend(x_tile_tasks(x, TT + tt * 128, g1T, uT, tt * 128, [B_uT[tt]]))
                    ilv = Interleaver(lagged(nxt), 32)
                for cb in range(4):
                    s = slab_use(("w2", fb, cb))
                    for tt in range(8):
                        bk = (rot["big"] % 8) if ilv is None else (4 + rot["big"] % 4)
                        rot["big"] += 1
                        mm_group(bk, ps[bk][:, :], [(yT[:, k, tt * 128:(tt + 1) * 128], slab[s][:, k, :]) for k in range(KC)], reads=[B_slab[s]] + B_yTc)
                        P.emit("dve", lambda e, tt=tt, cb=cb, bk=bk: e.tensor_tensor(out=xres[:, tt, cb * 512:(cb + 1) * 512], in0=xres[:, tt, cb * 512:(cb + 1) * 512],
                                                                                   in1=ps[bk][:, :], op=ALU.add),
                               reads=[B_ps[bk], B_xres[tt]], writes=[B_xres[tt]])
                        if ilv is not None:
                            ilv.tick()
            if ilv is not None:
                ilv.flush()
            for fi, tt in enumerate((3, 4, 5, 6, 0, 1, 2, 7)):
                if fi == 4 and t == 0:
                    P.barrier(extra=store_ops)
                i = tt % 2
                P.emit("act", lambda e, tt=tt, i=i: e.activation(out=xb[1][:], in_=xres[:, tt, :], func=AF.Square, accum_out=st_ss[i]),
                       reads=[B_xres[tt]], writes=[B_xb[1], B_st[i]])
                P.emit("act", lambda e, i=i: e.activation(out=st_rs[i], in_=st_ss[i], func=AF.Ln, scale=1.0 / D, bias=EPS), reads=[B_st[i]], writes=[B_st[i]])
                P.emit("act", lambda e, i=i: e.activation(out=st_rs[i], in_=st_rs[i], func=AF.Exp, scale=-0.5), reads=[B_st[i]], writes=[B_st[i]])
                P.emit("dve", lambda e, tt=tt, i=i: e.scalar_tensor_tensor(out=xres[:, tt, :], in0=xres[:, tt, :], scalar=st_rs[i], in1=gf_rep[:],
                                                                          op0=ALU.mult, op1=ALU.mult),
                       reads=[B_xres[tt], B_st[i], B_const], writes=[B_xres[tt]])
                r0 = t * TT + tt * 128
                store_ops.append(P.emit("sp", lambda e, tt=tt, r0=r0: e.dma_start(out=out[r0:r0 + 128, :], in_=xres[:, tt, :]),
                                        reads=[B_xres[tt]], dma_sem=sem_st[tt]))
        assert sl_state["use"] == len(specs)
        for i in range(8):
            k = id(sem_st[i])
            P.eng["sp"].prog.append(("wait", sem_st[i], P.dma_count[k]))

        for n, fn in (("pe", block.tensor), ("act", block.scalar), ("dve", block.vector), ("pool", block.gpsimd), ("sp", block.sync)):
            fn(lambda e, n=n: P.run_engine(n, e))
    return nc


_NC_CACHE = {}


def _prep_inputs(inputs):
    f = lambda a: np.ascontiguousarray(np.asarray(a, dtype=np.float32))
    x = f(inputs["x"])
    shared = {
        "norm1_g": f(inputs["norm1_g"]).reshape(D),
        "w_in": f(inputs["w_in"]).reshape(D, DIN),
        "w_gate_up": f(inputs["w_gate_up"]).reshape(16, 512),
        "b_gate": f(inputs["b_gate"]).reshape(1, 512),
        "conv_w": f(inputs["conv_w"]).reshape(1024, 3),
        "conv_norm_g": f(inputs["conv_norm_g"]).reshape(1024),
        "gla_norm_g": f(inputs["gla_norm_g"]).reshape(1, 256),
        "w_out": f(inputs["w_out"]).reshape(D, D),
        "norm2_g": f(inputs["norm2_g"]).reshape(D),
        "w_ff1": f(inputs["w_ff1"]).reshape(D, DFF),
        "w_ff2": f(inputs["w_ff2"]).reshape(DFF, D),
        "norm_f_g": f(inputs["norm_f_g"]).reshape(1, D),
    }
    in_maps = []
    for c in range(NCORE):
        b, q = c // 4, c % 4
        m = dict(shared)
        m["x"] = np.ascontiguousarray(x[b, q * TOK:(q + 1) * TOK, :])
        xhalo = np.zeros((128, D), np.float32)
        if q > 0:
            xhalo[126:128] = x[b, q * TOK - 2:q * TOK, :]
        m["xh"] = xhalo
        cm = np.zeros((128, 8), np.float32)
        for r in range(NCORE):
            if r // 4 == b and r < c:
                cm[:, r] = 1.0
        m["cmask"] = cm
        in_maps.append(m)
    return in_maps


def kernel(**inputs):
    if "nc" not in _NC_CACHE:
        _NC_CACHE["nc"] = build_program()
    nc = _NC_CACHE["nc"]
    in_maps = _prep_inputs(inputs)
    res = run_bass_kernel_spmd(nc, in_maps, core_ids=list(range(NCORE)))
    outp = np.empty((2, 8192, D), np.float32)
    for c in range(NCORE):
        b, q = c // 4, c % 4
        outp[b, q * TOK:(q + 1) * TOK, :] = np.asarray(res.results[c]["out"], dtype=np.float32).reshape(TOK, D)
    return outp
```
